# Optimizing a Trainium2 kernel written in Bass

```python
import jax, jax.numpy as jnp
from jax import lax
import numpy as np

D_MODEL = 1024
BATCH = 2
SEQ = 8192
DEPTH = 4

GRID_W = 64
CTX_LEN = 256
HEAD_DIM = 64
NA_HEADS = 8
NA_WIN_H = 8
NA_WIN_W = 16
GQA_Q_HEADS = 8
GQA_KV_HEADS = 2
GQA_REP = GQA_Q_HEADS // GQA_KV_HEADS
NA_WIDTH = NA_HEADS * HEAD_DIM
GQA_Q_WIDTH = GQA_Q_HEADS * HEAD_DIM
GQA_KV_WIDTH = GQA_KV_HEADS * HEAD_DIM
IN_SIZES = (NA_WIDTH, NA_WIDTH, NA_WIDTH, GQA_Q_WIDTH, GQA_KV_WIDTH, GQA_KV_WIDTH, D_MODEL, D_MODEL)
IN_COLS = sum(IN_SIZES)
IN_SPLITS = tuple(int(v) for v in np.cumsum(IN_SIZES)[:-1])
D_FF = -(-(8 * D_MODEL) // (3 * 256)) * 256
Q_BLOCK = 128
ROPE_THETA = 10000.0
EPS = 1e-6
SCALE = HEAD_DIM ** -0.5

kernel_name = "hybrid_natten_gqa_prefix_dit"


def rms_norm(x, g):
    xf = x.astype(jnp.float32)
    y = xf * lax.rsqrt(jnp.mean(xf * xf, axis=-1, keepdims=True) + EPS)
    return (y * g.astype(jnp.float32)).astype(x.dtype)


def modulate(h, shift, scale):
    return h * (1 + scale) + shift


def heads(t, n_heads):
    b, n, _ = t.shape
    return t.reshape(b, n, n_heads, HEAD_DIM).transpose(0, 2, 1, 3)


def merge_heads(t):
    b, h, n, d = t.shape
    return t.transpose(0, 2, 1, 3).reshape(b, n, h * d)


def axial_rope_tables(n_tokens):
    t = jnp.arange(n_tokens)
    row = (t // GRID_W).astype(jnp.float32)
    col = (t % GRID_W).astype(jnp.float32)
    half = HEAD_DIM // 2
    inv = ROPE_THETA ** (-jnp.arange(0, half, 2, dtype=jnp.float32) / half)
    ang = jnp.concatenate([row[:, None] * inv, col[:, None] * inv], axis=-1)
    return jnp.cos(ang), jnp.sin(ang)


def apply_rope(x, cos, sin):
    xf = x.astype(jnp.float32).reshape(x.shape[:-1] + (HEAD_DIM // 2, 2))
    x0, x1 = xf[..., 0], xf[..., 1]
    out = jnp.stack([x0 * cos - x1 * sin, x0 * sin + x1 * cos], axis=-1)
    return out.reshape(x.shape).astype(x.dtype)


def na_window_indices(rows):
    kh = min(NA_WIN_H, rows)
    r = jnp.arange(rows)
    col = jnp.arange(GRID_W)
    r_start = jnp.clip(r - kh // 2, 0, rows - kh)
    row_idx = r_start[:, None] + jnp.arange(kh)[None, :]
    c_start = jnp.clip(col - NA_WIN_W // 2, 0, GRID_W - NA_WIN_W)
    in_win = (col[None, :] >= c_start[:, None]) & (col[None, :] < c_start[:, None] + NA_WIN_W)
    dr_idx = row_idx - r[:, None] + (NA_WIN_H - 1)
    dc = col[None, :] - col[:, None]
    dc_idx = jnp.clip(dc, -(NA_WIN_W - 1), NA_WIN_W - 1) + (NA_WIN_W - 1)
    return row_idx, in_win, dr_idx, dc_idx


def na_bias(rpb, in_win, dr_idx, dc_idx):
    b = rpb.astype(jnp.float32)[:, dr_idx[:, None, :, None], dc_idx[None, :, None, :]]
    return jnp.where(in_win[None, None, :, None, :], b, -jnp.inf)


def na_latent(q, k, v, k_ctx, v_ctx, bias, row_idx):
    b, h, s, d = q.shape
    rows = s // GRID_W
    kh = row_idx.shape[1]
    qg = q.reshape(b, h, rows, GRID_W, d)
    kg = k.reshape(b, h, rows, GRID_W, d)[:, :, row_idx]
    vg = v.reshape(b, h, rows, GRID_W, d)[:, :, row_idx]
    s_win = jnp.einsum('bhrwd,bhrjud->bhrwju', qg, kg).astype(jnp.float32) * SCALE + bias[None]
    s_win = s_win.reshape(b, h, rows, GRID_W, kh * GRID_W)
    s_ctx = jnp.einsum('bhrwd,bhld->bhrwl', qg, k_ctx).astype(jnp.float32) * SCALE
    p = jax.nn.softmax(jnp.concatenate([s_win, s_ctx], axis=-1), axis=-1).astype(v.dtype)
    p_win = p[..., :kh * GRID_W].reshape(b, h, rows, GRID_W, kh, GRID_W)
    p_ctx = p[..., kh * GRID_W:]
    o = jnp.einsum('bhrwju,bhrjud->bhrwd', p_win, vg) + jnp.einsum('bhrwl,bhld->bhrwd', p_ctx, v_ctx)
    return o.reshape(b, h, s, d)


def softmax_attn(q, k, v):
    s = jnp.einsum('bgrqd,bgkd->bgrqk', q, k).astype(jnp.float32) * SCALE
    p = jax.nn.softmax(s, axis=-1).astype(v.dtype)
    return jnp.einsum('bgrqk,bgkd->bgrqd', p, v)


def gqa_latent(q, k_all, v_all):
    b, g, r, s, d = q.shape
    nb = s // Q_BLOCK
    qb = q.reshape(b, g, r, nb, Q_BLOCK, d).transpose(3, 0, 1, 2, 4, 5)
    o = lax.map(lambda qblk: softmax_attn(qblk, k_all, v_all), qb)
    return o.transpose(1, 2, 3, 0, 4, 5).reshape(b, g, r, s, d)


def branch_merge(ya, yb, ga, gb, w_pa, w_pb, w_o):
    merged = jax.nn.sigmoid(ga) * (ya @ w_pa) + jax.nn.sigmoid(gb) * (yb @ w_pb)
    return merged @ w_o


def swiglu(h, w_in, w_out):
    a, u = jnp.split(h @ w_in, 2, axis=-1)
    return (jax.nn.silu(a) * u) @ w_out


def setup_inputs(seed: int = 0) -> dict:
    key = jax.random.key(seed)
    ks = jax.random.split(key, 18)
    f32 = jnp.float32
    nrm = lambda k, shp, sc: jax.random.normal(k, shp, f32) * sc
    return {
        "x": nrm(ks[0], (BATCH, SEQ, D_MODEL), 1.0),
        "c": nrm(ks[1], (BATCH, D_MODEL), 1.0),
        "ctx": nrm(ks[2], (BATCH, CTX_LEN, D_MODEL), 1.0),
        "c_ctx": nrm(ks[3], (D_MODEL,), 1.0),
        "w_mod": nrm(ks[4], (DEPTH, D_MODEL, 6 * D_MODEL), D_MODEL ** -0.5),
        "b_mod": nrm(ks[5], (DEPTH, 6 * D_MODEL), 0.01),
        "norm1": 1.0 + nrm(ks[6], (DEPTH, D_MODEL), 0.02),
        "w_in": nrm(ks[7], (DEPTH, D_MODEL, IN_COLS), D_MODEL ** -0.5),
        "na_rpb": nrm(ks[8], (DEPTH, NA_HEADS, 2 * NA_WIN_H - 1, 2 * NA_WIN_W - 1), 0.1),
        "q_gain": 1.0 + nrm(ks[9], (DEPTH, HEAD_DIM), 0.02),
        "k_gain": 1.0 + nrm(ks[10], (DEPTH, HEAD_DIM), 0.02),
        "w_pa": nrm(ks[11], (DEPTH, NA_WIDTH, D_MODEL), NA_WIDTH ** -0.5),
        "w_pb": nrm(ks[12], (DEPTH, GQA_Q_WIDTH, D_MODEL), GQA_Q_WIDTH ** -0.5),
        "w_o": nrm(ks[13], (DEPTH, D_MODEL, D_MODEL), D_MODEL ** -0.5),
        "norm2": 1.0 + nrm(ks[14], (DEPTH, D_MODEL), 0.02),
        "w_ffn_in": nrm(ks[15], (DEPTH, D_MODEL, 2 * D_FF), D_MODEL ** -0.5),
        "w_ffn_out": nrm(ks[16], (DEPTH, D_FF, D_MODEL), D_FF ** -0.5),
        "final_norm": 1.0 + nrm(ks[17], (D_MODEL,), 0.02),
    }


def reference(x, c, ctx, c_ctx, w_mod, b_mod, norm1, w_in, na_rpb, q_gain, k_gain, w_pa, w_pb, w_o,
              norm2, w_ffn_in, w_ffn_out, final_norm):
    b, s, _ = x.shape
    n_ctx = ctx.shape[1]
    rows = s // GRID_W
    cos, sin = axial_rope_tables(s)
    row_idx, in_win, dr_idx, dc_idx = na_window_indices(rows)
    silu_c = jax.nn.silu(c)
    silu_cc = jax.nn.silu(c_ctx)

    for l in range(DEPTH):
        last = l == DEPTH - 1
        mod = silu_c @ w_mod[l] + b_mod[l]
        mod_c = silu_cc @ w_mod[l] + b_mod[l]
        sh1, sc1, g1, sh2, sc2, g2 = jnp.split(mod[:, None, :], 6, axis=-1)
        csh1, csc1, cg1, csh2, csc2, cg2 = jnp.split(mod_c, 6)

        h = modulate(rms_norm(x, norm1[l]), sh1, sc1)
        hc = modulate(rms_norm(ctx, norm1[l]), csh1, csc1)
        na_q, na_k, na_v, gq, gk, gv, ga, gb = jnp.split(h @ w_in[l], IN_SPLITS, axis=-1)
        na_qc, na_kc, na_vc, gqc, gkc, gvc, gac, gbc = jnp.split(hc @ w_in[l], IN_SPLITS, axis=-1)

        qa, ka, va = heads(na_q, NA_HEADS), heads(na_k, NA_HEADS), heads(na_v, NA_HEADS)
        qa_c, ka_c, va_c = heads(na_qc, NA_HEADS), heads(na_kc, NA_HEADS), heads(na_vc, NA_HEADS)
        bias = na_bias(na_rpb[l], in_win, dr_idx, dc_idx)
        ya = merge_heads(na_latent(qa, ka, va, ka_c, va_c, bias, row_idx))

        qb = apply_rope(rms_norm(heads(gq, GQA_Q_HEADS), q_gain[l]), cos, sin)
        kb = apply_rope(rms_norm(heads(gk, GQA_KV_HEADS), k_gain[l]), cos, sin)
        vb = heads(gv, GQA_KV_HEADS)
        qb_c = rms_norm(heads(gqc, GQA_Q_HEADS), q_gain[l])
        kb_c = rms_norm(heads(gkc, GQA_KV_HEADS), k_gain[l])
        vb_c = heads(gvc, GQA_KV_HEADS)
        k_all = jnp.concatenate([kb, kb_c], axis=2)
        v_all = jnp.concatenate([vb, vb_c], axis=2)
        ob = gqa_latent(qb.reshape(b, GQA_KV_HEADS, GQA_REP, s, HEAD_DIM), k_all, v_all)
        yb = merge_heads(ob.reshape(b, GQA_Q_HEADS, s, HEAD_DIM))

        x = x + g1 * branch_merge(ya, yb, ga, gb, w_pa[l], w_pb[l], w_o[l])

        if not last:
            ya_c = merge_heads(softmax_attn(qa_c[:, :, None], ka_c, va_c)[:, :, 0])
            ob_c = softmax_attn(qb_c.reshape(b, GQA_KV_HEADS, GQA_REP, n_ctx, HEAD_DIM), kb_c, vb_c)
            yb_c = merge_heads(ob_c.reshape(b, GQA_Q_HEADS, n_ctx, HEAD_DIM))
            ctx = ctx + cg1 * branch_merge(ya_c, yb_c, gac, gbc, w_pa[l], w_pb[l], w_o[l])

        h2 = modulate(rms_norm(x, norm2[l]), sh2, sc2)
        x = x + g2 * swiglu(h2, w_ffn_in[l], w_ffn_out[l])
        if not last:
            hc2 = modulate(rms_norm(ctx, norm2[l]), csh2, csc2)
            ctx = ctx + cg2 * swiglu(hc2, w_ffn_in[l], w_ffn_out[l])

    return rms_norm(x, final_norm)
```

```python
import numpy as np
import concourse.bass as bass
import concourse.mybir as mybir
from concourse.bass_utils import run_bass_kernel_spmd

F32 = mybir.dt.float32
BF16 = mybir.dt.bfloat16
AF = mybir.ActivationFunctionType
ALU = mybir.AluOpType

NCORES = 8
D = 1024
KC = 8
DEPTH = 4
T_OWN = 2048
T_CTX = 256
TT = T_OWN + T_CTX
GRID_W = 64
HD = 64
DFF = 2816
FC = DFF // 128
IN_COLS = 4352
NEG = -30000.0
EPS = 1e-6
CH = [(0, 512), (512, 512), (1024, 512), (1536, 512), (2048, 256)]
BLK = [[0, 1], [2, 3, 4]]
XROW = 2080


class Buf:
    __slots__ = ("name", "w", "r")

    def __init__(self, name):
        self.name = name
        self.w = {}
        self.r = {}


class DSem:
    __slots__ = ("sem", "cnt", "key")

    def __init__(self, nc, name):
        self.sem = nc.alloc_semaphore(name)
        self.cnt = 0
        self.key = name


class Eng:
    def __init__(self, nc, name, h):
        self.name = name
        self.h = h
        self.key = "E_" + name
        self.sem = nc.alloc_semaphore("sem_" + name)
        self.cnt = 0
        self.waited = {}


class KB:
    def __init__(self, nc):
        self.nc = nc
        self.E = {
            "pe": Eng(nc, "pe", nc.tensor),
            "act": Eng(nc, "act", nc.scalar),
            "dve": Eng(nc, "dve", nc.vector),
            "pool": Eng(nc, "pool", nc.gpsimd),
            "sp": Eng(nc, "sp", nc.sync),
        }
        self.dsems = []
        self.n_wait = 0
        self.n_ins = 0

    def dsem(self, name):
        d = DSem(self.nc, "D_" + name)
        self.dsems.append(d)
        return d

    @staticmethod
    def _merge(need, d):
        for key, sv in d.items():
            if key not in need or need[key][1] < sv[1]:
                need[key] = sv

    def _wait(self, e, reads, writes, skip_key=None):
        need = {}
        for b in reads:
            self._merge(need, b.w)
        oth = {}
        for b in writes:
            self._merge(oth, b.w)
            self._merge(oth, b.r)
        for key, sv in oth.items():
            if key not in need or need[key][1] < sv[1]:
                need[key] = sv
        for key, (sem, val) in need.items():
            if key == e.key and e.name == "pe":
                continue
            if key == skip_key:
                continue
            if e.waited.get(key, 0) >= val:
                continue
            e.h.wait_ge(sem, val)
            e.waited[key] = val
            self.n_wait += 1

    def op(self, eng, fn, reads=(), writes=()):
        e = self.E[eng]
        self._wait(e, reads, writes)
        ins = fn(e.h)
        e.cnt += 1
        ins.then_inc(e.sem, 1)
        self.n_ins += 1
        rec = (e.sem, e.cnt)
        for b in reads:
            b.r[e.key] = rec
        for b in writes:
            b.w[e.key] = rec
        return ins

    def dma(self, q, ds, out, in_, reads=(), writes=()):
        e = self.E[q]
        self._wait(e, reads, writes, skip_key=ds.key)
        ins = e.h.dma_start(out=out, in_=in_)
        ds.cnt += 16
        ins.then_inc(ds.sem, 16)
        self.n_ins += 1
        rec = (ds.sem, ds.cnt)
        for b in reads:
            b.r[ds.key] = rec
        for b in writes:
            b.w[ds.key] = rec
        return ins

    def seal(self, ds, bufs):
        rec = (ds.sem, ds.cnt)
        for b in bufs:
            if ds.key in b.w:
                b.w[ds.key] = rec
            if ds.key in b.r:
                b.r[ds.key] = rec

    def barrier(self):
        for e in self.E.values():
            for f in self.E.values():
                if f is e or f.cnt == 0 or f.name == "sp":
                    continue
                if e.waited.get(f.key, 0) < f.cnt:
                    e.h.wait_ge(f.sem, f.cnt)
                    e.waited[f.key] = f.cnt
                    self.n_wait += 1
            for d in self.dsems:
                if d.cnt and e.waited.get(d.key, 0) < d.cnt:
                    e.h.wait_ge(d.sem, d.cnt)
                    e.waited[d.key] = d.cnt
                    self.n_wait += 1


def build_program(n_layers=DEPTH, debug=None):
    nc = bass.Bass("TRN2", target_bir_lowering=False)
    k = KB(nc)

    def din(name, shape, dt=F32):
        return nc.dram_tensor(name, list(shape), dt, kind="ExternalInput").ap()

    x_in = din("xT_in", [D, TT])
    cvec_in = din("cvec", [128, KC, 2])
    norm1_in = din("norm1T", [128, DEPTH, KC])
    norm2_in = din("norm2T", [128, DEPTH, KC])
    bmod_in = din("bmodT", [128, DEPTH, 48])
    fn_in = din("fnT", [128, KC])
    qg_in = din("qgainT", [128, DEPTH])
    kg_in = din("kgainT", [128, DEPTH])
    cos_in = din("cosT", [128, T_OWN])
    sin_in = din("sinT", [128, T_OWN])
    perm_in = din("perm", [128, 128])
    ident_in = din("ident", [128, 128])
    onehot_in = din("onehot", [128, 8 * 128])
    pen_in = din("pen", [128, 4 * 512])
    sel_in = din("sel", [128, 8])
    tb_in = din("tb", [DEPTH, 4, 128, 2 * 22 * 64])
    w_mod = din("w_mod", [n_layers, D, 6 * D])
    w_in = din("w_in", [n_layers, D, IN_COLS])
    w_pa = din("w_pa", [n_layers, 512, D])
    w_pb = din("w_pb", [n_layers, 512, D])
    w_o = din("w_o", [n_layers, D, D])
    w_f1 = din("w_ffn_in", [n_layers, D, 2 * DFF])
    w_f2 = din("w_ffn_out", [n_layers, DFF, D])
    y_out = nc.dram_tensor("yT_out", [D, T_OWN], F32, kind="ExternalOutput").ap()
    dbg_names = []

    x_home = nc.dram_tensor("x_home", [128, KC * TT], F32)
    XW = [2048, XROW, 2048, XROW]
    xin = [nc.dram_tensor("xchg_in%d" % p, [128, XW[p]], BF16) for p in range(4)]
    xout = [nc.dram_tensor("xchg_out%d" % p, [4 * 128, XW[p]], BF16) for p in range(4)]

    base = (nc.sbuf_base + 63) // 64 * 64
    total = (nc.sbuf_top - base) // 64 * 64
    nc.alloc_sbuf_tensor("arena", [128, total], mybir.dt.uint8)

    def at(name, shape, dt, off):
        return nc.alloc_sbuf_tensor_at(name, list(shape), dt, offset=base + off)

    A = 0
    xT = at("xT", [128, KC, TT], F32, A)
    kT_na = at("kT_na", [128, 4, 2816], BF16, A)
    v_na = at("v_na", [128, 22, 520], BF16, A + 22528)
    kT_g = at("kT_g", [128, 8448], BF16, A)
    v_g = at("v_g", [128, 66, 130], BF16, A + 16896)
    cosT = at("cosT", [128, T_OWN], F32, A + 45440)
    sinT = at("sinT", [128, T_OWN], F32, A + 53632)
    kst = at("kst", [128, TT], BF16, A + 61824)
    vst = at("vst", [128, 18, 130], BF16, A + 66432)
    C0 = 73728
    co = [C0]

    def cat(name, shape, dt, nbytes):
        t = at(name, shape, dt, co[0])
        co[0] += (nbytes + 63) // 64 * 64
        return t

    ones_bf = cat("ones_bf", [128, 128], BF16, 256)
    bd_bf = cat("bd_bf", [128, 128], BF16, 256)
    ident_bf = cat("ident_bf", [128, 128], BF16, 256)
    sel64_bf = cat("sel64_bf", [128, 128], BF16, 256)
    perm_f = cat("perm_f", [128, 128], F32, 512)
    onehot = cat("onehot", [128, 8 * 128], BF16, 2048)
    pen = cat("pen", [128, 4 * 512], BF16, 4096)
    sel = cat("sel", [128, 8], F32, 32)
    eps_t = cat("eps_t", [128, 1], F32, 4)
    onecol_f = cat("onecol_f", [128, 2], F32, 8)
    norm1T = cat("norm1T", [128, DEPTH, KC], F32, 128)
    norm2T = cat("norm2T", [128, DEPTH, KC], F32, 128)
    bmodT = cat("bmodT", [128, DEPTH, 48], F32, 768)
    fnT = cat("fnT", [128, KC], F32, 32)
    qgT = cat("qgT", [128, DEPTH], F32, 16)
    kgT = cat("kgT", [128, DEPTH], F32, 16)
    cvec = cat("cvec", [128, KC, 2], F32, 64)
    silu_c = cat("silu_c", [128, KC, 2], F32, 64)
    modT_l = [cat("modT0", [128, 48, 2], F32, 384), cat("modT1", [128, 48, 2], F32, 384)]
    A1_l = [cat("A1_0", [128, KC, 2], F32, 64), cat("A1_1", [128, KC, 2], F32, 64)]
    A2_l = [cat("A2_0", [128, KC, 2], F32, 64), cat("A2_1", [128, KC, 2], F32, 64)]
    cur = {}
    assert co[0] <= C0 + 10240, co[0]
    HT = C0 + 10240
    hT = at("hT", [128, KC, TT], BF16, HT)
    Y = HT + 36864
    qn = at("qn", [128, 4, TT], BF16, Y)
    qg = at("qg", [128, 4, TT], BF16, Y + 18432)
    FR = Y + 36864
    W = total - 33792
    FSZ = W - FR
    assert FSZ >= 21248, FSZ

    def fat(name, shape, dt, off):
        return at(name, shape, dt, FR + off)

    stage = [at("stage0", [128, 2816], F32, W), at("stage1", [128, 2816], F32, W + 11264)]
    slot = [at("slot0", [128, 2816], BF16, W + 22528), at("slot1", [128, 2816], BF16, W + 28160)]
    tbias = [at("tbias0", [128, 2, 22, 64], F32, W), at("tbias1", [128, 2, 22, 64], F32, W + 11264)]
    tb16 = [at("tb16_0", [128, 2, 22, 64], BF16, W + 22528), at("tb16_1", [128, 2, 22, 64], BF16, W + 28160)]
    cstage_f = at("cstage_f", [128, 4 * 512], F32, W)

    psall = nc.alloc_psum_tensor("psall", [128, 8 * 512], F32)
    ps = [psall[:, i * 512:(i + 1) * 512] for i in range(8)]
    PSB = [Buf("ps%d" % i) for i in range(8)]

    B = {}

    def bf(name):
        if name not in B:
            B[name] = Buf(name)
        return B[name]

    d_const = k.dsem("const")
    d_x = k.dsem("x")
    d_stage = [k.dsem("st0"), k.dsem("st1")]
    d_tb = [k.dsem("tb0"), k.dsem("tb1")]
    d_xh = k.dsem("xhome")
    d_xin = k.dsem("xin")
    d_kv = k.dsem("kvload")
    d_halo = [k.dsem("halo0"), k.dsem("halo1")]
    d_rope = k.dsem("rope")
    d_out = k.dsem("out")
    ccsem = nc.alloc_semaphore("ccsem")
    cc_cnt = [0]

    wl_state = {"i": 0}

    def wload(srcs, shape, tag):
        i = wl_state["i"] % 2
        wl_state["i"] += 1
        n = shape[1] * shape[2]
        st_v = stage[i][:, 0:n].rearrange("p (a b) -> p a b", a=shape[1])
        sl_v = slot[i][:, 0:n].rearrange("p (a b) -> p a b", a=shape[1])
        sb, lb = bf("stage%d" % i), bf("slot%d" % i)
        for fn, src in srcs:
            k.dma("sp", d_stage[i], fn(st_v), src, writes=[sb])
        k.seal(d_stage[i], [sb])
        k.op("dve", lambda e: e.tensor_copy(out=slot[i][:, 0:n], in_=stage[i][:, 0:n]), reads=[sb], writes=[lb])
        return sl_v, lb

    def w_kc(wap, c0, ncols):
        return wap[:, c0:c0 + ncols].rearrange("(kc p) n -> p kc n", p=128)

    def full(v):
        return v

    psr = {"i": 0}

    def next_ps(lo=0, hi=8):
        i = lo + psr["i"] % (hi - lo)
        psr["i"] += 1
        return i

    def colsel(c):
        return 1 if c == 4 else 0

    cb = bf("consts")
    k.dma("sp", d_const, cvec[:], cvec_in, writes=[cb])
    k.dma("sp", d_const, norm1T[:], norm1_in, writes=[cb])
    k.dma("sp", d_const, norm2T[:], norm2_in, writes=[cb])
    k.dma("sp", d_const, bmodT[:], bmod_in, writes=[cb])
    k.dma("sp", d_const, fnT[:], fn_in, writes=[cb])
    k.dma("sp", d_const, qgT[:], qg_in, writes=[cb])
    k.dma("sp", d_const, kgT[:], kg_in, writes=[cb])
    k.dma("sp", d_const, perm_f[:], perm_in, writes=[cb])
    k.dma("sp", d_const, sel[:], sel_in, writes=[cb])
    k.dma("sp", d_const, cstage_f[:, :], pen_in, writes=[cb, bf("stage0")])
    k.seal(d_const, [cb, bf("stage0")])
    k.op("dve", lambda e: e.tensor_copy(out=pen[:, :], in_=cstage_f[:, :]), reads=[cb, bf("stage0")], writes=[bf("pen")])
    k.dma("sp", d_const, cstage_f[:, 0:1024], onehot_in, reads=[bf("pen")], writes=[bf("stage0")])
    k.op("dve", lambda e: e.tensor_copy(out=onehot[:, :], in_=cstage_f[:, 0:1024]), reads=[bf("stage0")], writes=[bf("onehot")])
    k.dma("sp", d_const, cstage_f[:, 0:128], ident_in, reads=[bf("onehot")], writes=[bf("stage0")])
    k.op("dve", lambda e: e.tensor_copy(out=ident_bf[:, :], in_=cstage_f[:, 0:128]), reads=[bf("stage0")], writes=[bf("onehot")])
    k.op("pool", lambda e: e.memset(ones_bf[:], 1.0), writes=[cb])
    k.op("pool", lambda e: e.memset(eps_t[:], EPS), writes=[cb])
    k.op("pool", lambda e: e.memset(onecol_f[:], 1.0), writes=[cb])
    k.op("pool", lambda e: e.memset(sel64_bf[:], 0.0), writes=[cb])
    k.op("pool", lambda e: e.memset(sel64_bf[64:65, :], 1.0), writes=[cb])
    k.op("pool", lambda e: e.memset(bd_bf[:], 0.0), writes=[cb])
    k.op("pool", lambda e: e.memset(bd_bf[0:64, 0:64], 1.0), writes=[cb])
    k.op("pool", lambda e: e.memset(bd_bf[64:128, 64:128], 1.0), writes=[cb])
    k.op("act", lambda e: e.activation(out=silu_c[:], in_=cvec[:], func=AF.Silu), reads=[cb], writes=[cb])
    k.op("dve", lambda e: e.tensor_scalar(out=qgT[:], in0=qgT[:], scalar1=0.125, scalar2=None, op0=ALU.mult),
         reads=[cb], writes=[cb])
    XB = [bf("x%d" % c) for c in range(5)]
    for c, (t0, tw) in enumerate(CH):
        k.dma("sp", d_x, xT[:, :, t0:t0 + tw], x_in[:, t0:t0 + tw].rearrange("(kc p) t -> p kc t", p=128), writes=[XB[c]])
    k.seal(d_x, XB)
    k.barrier()

    def set_layer(l):
        cur["modT"], cur["A1"], cur["A2"], cur["mb"] = modT_l[l % 2], A1_l[l % 2], A2_l[l % 2], bf("mod%d" % (l % 2))

    MODB = 7

    mstage = [stage[0], stage[1]]
    mtok = ["stage0", "stage1"]
    mds = [d_stage[0], d_stage[1]]
    mstA = [at("mstA0", [128, 2048], F32, A + 45440), at("mstA1", [128, 2048], F32, A + 45440 + 8192)]
    d_mst = [k.dsem("mst0"), k.dsem("mst1")]

    maccW = [at("maccW0", [128, 2, 256], F32, W + 8192), at("maccW1", [128, 2, 256], F32, W + 19456)]
    maccA = [at("maccA0", [128, 2, 256], F32, A + 61824), at("maccA1", [128, 2, 256], F32, A + 61824 + 2048)]
    macc = [maccW[0], maccW[1]]
    macct = ["maccW0", "maccW1"]

    def mod_use_alt(alt):
        if alt:
            mstage[:] = mstA; mtok[:] = ["mstA0", "mstA1"]; mds[:] = d_mst
            macc[:] = maccA; macct[:] = ["maccA0", "maccA1"]
        else:
            mstage[:] = [stage[0], stage[1]]; mtok[:] = ["stage0", "stage1"]; mds[:] = [d_stage[0], d_stage[1]]
            macc[:] = maccW; macct[:] = ["maccW0", "maccW1"]

    def mod_dma(l, g):
        i = g % 2
        st_v = mstage[i][:, 0:2048].rearrange("p (a b) -> p a b", a=8)
        k.dma("sp", mds[i], st_v, w_kc(w_mod[l], g * 256, 256), writes=[bf(mtok[i])])

    def mod_mm(l, g):
        i = g % 2
        sb = bf(mtok[i])
        ab = bf(macct[i])
        st_v = mstage[i][:, 0:2048].rearrange("p (a b) -> p a b", a=8)
        acc = macc[i]
        for v in range(2):
            k.op("dve", lambda e, v=v: e.tensor_scalar(out=acc[:, v, :], in0=st_v[:, 0, :], scalar1=silu_c[:, 0, v:v + 1],
                                                     scalar2=None, op0=ALU.mult), reads=[sb, cb], writes=[ab])
            for kc in range(1, KC):
                k.op("dve", lambda e, v=v, kc=kc: e.scalar_tensor_tensor(
                    out=acc[:, v, :], in0=st_v[:, kc, :], scalar=silu_c[:, kc, v:v + 1], in1=acc[:, v, :],
                    op0=ALU.mult, op1=ALU.add), reads=[sb, cb, ab], writes=[ab])
        for cc in range(2):
            ch = g * 2 + cc
            for v in range(2):
                k.op("pe", lambda e, cc=cc, ch=ch, v=v: e.matmul(
                    ps[MODB][:, ch * 2 + v:ch * 2 + v + 1], lhsT=acc[:, v, cc * 128:(cc + 1) * 128], rhs=onecol_f[:, 0:1],
                    start=True, stop=True), reads=[ab, cb], writes=[PSB[MODB]])

    def mod_finish(l):
        mT, a1, a2, mb = modT_l[l % 2], A1_l[l % 2], A2_l[l % 2], bf("mod%d" % (l % 2))
        for col in range(2):
            k.op("dve", lambda e, col=col: e.tensor_tensor(
                out=mT[:, :, col], in0=ps[MODB][:, 0:96].rearrange("p (c t) -> p c t", t=2)[:, :, col],
                in1=bmodT[:, l, :], op=ALU.add), reads=[PSB[MODB], cb], writes=[mb])
        for col in range(2):
            k.op("dve", lambda e, col=col: e.scalar_tensor_tensor(
                out=a1[:, :, col], in0=mT[:, 8:16, col], scalar=1.0, in1=norm1T[:, l, :], op0=ALU.add, op1=ALU.mult),
                reads=[mb, cb], writes=[mb])
            k.op("dve", lambda e, col=col: e.scalar_tensor_tensor(
                out=a2[:, :, col], in0=mT[:, 32:40, col], scalar=1.0, in1=norm2T[:, l, :], op0=ALU.add, op1=ALU.mult),
                reads=[mb, cb], writes=[mb])

    def phase_mod(l):
        mod_dma(l, 0)
        for g in range(24):
            if g + 1 < 24:
                mod_dma(l, g + 1)
            mod_mm(l, g)
        mod_finish(l)

    def norm_chunk(c, Aw, shift_idx, out_t, out_off, tagbuf):
        t0, tw = CH[c]
        col = colsel(c)
        mb = cur["mb"]
        modT = cur["modT"]
        sq = fat("sq", [128, KC, 512], BF16, 0)
        rstd = fat("rstd", [128, 512], F32, 8192)
        tmp = [fat("ntmp0", [128, 512], F32, 10240), fat("ntmp1", [128, 512], F32, 12288)]
        sqb, rb = bf("sq"), bf("rstd")
        tb_ = [bf("ntmp0"), bf("ntmp1")]
        k.op("act", lambda e: e.activation(out=sq[:, :, 0:tw], in_=xT[:, :, t0:t0 + tw], func=AF.Square),
             reads=[XB[c]], writes=[sqb])
        pb = 7
        for kc in range(KC):
            k.op("pe", lambda e, kc=kc: e.matmul(ps[pb][:, 0:tw], lhsT=ones_bf[:], rhs=sq[:, kc, 0:tw],
                                                 start=(kc == 0), stop=(kc == KC - 1)),
                 reads=[sqb, cb], writes=[PSB[pb]])
        k.op("act", lambda e: e.activation(out=rstd[:, 0:tw], in_=ps[pb][:, 0:tw], func=AF.Sqrt, scale=1.0 / D,
                                           bias=eps_t[:, 0:1]), reads=[PSB[pb], cb], writes=[rb])
        k.op("dve", lambda e: e.reciprocal(out=rstd[:, 0:tw], in_=rstd[:, 0:tw]), reads=[rb], writes=[rb])
        for kc in range(KC):
            i = kc % 2
            k.op("dve", lambda e, kc=kc, i=i: e.scalar_tensor_tensor(
                out=tmp[i][:, 0:tw], in0=xT[:, kc, t0:t0 + tw], scalar=Aw[:, kc, col:col + 1], in1=rstd[:, 0:tw],
                op0=ALU.mult, op1=ALU.mult), reads=[XB[c], mb, rb], writes=[tb_[i]])
            k.op("act", lambda e, kc=kc, i=i: e.activation(
                out=out_t[:, kc, out_off:out_off + tw], in_=tmp[i][:, 0:tw], func=AF.Identity,
                bias=modT[:, shift_idx * 8 + kc, col:col + 1], scale=1.0), reads=[tb_[i], mb], writes=[tagbuf])

    HB = [bf("h%d" % c) for c in range(5)]

    def phase_norm1(l):
        for c in range(5):
            norm_chunk(c, cur["A1"], 0, hT, CH[c][0], HB[c])
            spill_chunk(c)

    def spill_chunk(c):
        t0, tw = CH[c]
        k.dma("sp", d_xh, x_home.ap().rearrange("p (kc t) -> p kc t", kc=KC)[:, :, t0:t0 + tw], xT[:, :, t0:t0 + tw],
              reads=[XB[c]], writes=[bf("xhome")])

    def phase_spill():
        pass

    def phase_restore():
        for c, (t0, tw) in enumerate(CH):
            k.dma("sp", d_x, xT[:, :, t0:t0 + tw], x_home.ap().rearrange("p (kc t) -> p kc t", kc=KC)[:, :, t0:t0 + tw],
                  reads=[bf("xhome")], writes=[XB[c]])
        k.seal(d_x, XB)

    def proj_fm(sl_v, lb, cc, c, pb):
        t0, tw = CH[c]
        for kc in range(KC):
            k.op("pe", lambda e, kc=kc: e.matmul(ps[pb][:, 0:tw], lhsT=sl_v[:, kc, cc * 128:(cc + 1) * 128],
                                                 rhs=hT[:, kc, t0:t0 + tw], start=(kc == 0), stop=(kc == KC - 1)),
                 reads=[lb, HB[c]], writes=[PSB[pb]])

    nr = {"i": 0, "q": []}

    def gqa_normrope(l, pb, c, gain_t, out_ap, out_buf, is_q):
        t0, tw = CH[c]
        i = nr["i"] % 2
        nr["i"] += 1
        o = i * 9216
        rg = fat("rg%d" % i, [128, 512], F32, o)
        sqh = fat("sqh%d" % i, [128, 512], BF16, o + 2048)
        rs_ = fat("rs_%d" % i, [128, 512], F32, o + 3072)
        t1 = fat("t1%d" % i, [128, 512], F32, o + 5120)
        t2 = fat("t2%d" % i, [128, 512], F32, o + 7168)
        rgb, sqb, rsb, t1b, t2b = bf("rg%d" % i), bf("sqh%d" % i), bf("rs_%d" % i), bf("t1%d" % i), bf("t2%d" % i)
        rp = bf("rope")

        def s1():
            k.op("act", lambda e: e.activation(out=rg[:, 0:tw], in_=ps[pb][:, 0:tw], func=AF.Copy, scale=gain_t[:, l:l + 1]),
                 reads=[PSB[pb], cb], writes=[rgb])
            k.op("act", lambda e: e.activation(out=sqh[:, 0:tw], in_=ps[pb][:, 0:tw], func=AF.Square),
                 reads=[PSB[pb]], writes=[sqb])

        def s2():
            k.op("pe", lambda e: e.matmul(ps[6][:, 0:tw], lhsT=bd_bf[:], rhs=sqh[:, 0:tw], start=True, stop=True),
                 reads=[sqb, cb], writes=[PSB[6]])
            k.op("act", lambda e: e.activation(out=rs_[:, 0:tw], in_=ps[6][:, 0:tw], func=AF.Sqrt, scale=1.0 / HD,
                                               bias=eps_t[:, 0:1]), reads=[PSB[6], cb], writes=[rsb])
            k.op("dve", lambda e: e.reciprocal(out=rs_[:, 0:tw], in_=rs_[:, 0:tw]), reads=[rsb], writes=[rsb])
            if c != 4:
                k.op("pe", lambda e: e.matmul(ps[7][:, 0:tw], lhsT=perm_f[:], rhs=rg[:, 0:tw], start=True, stop=True),
                     reads=[rgb, cb], writes=[PSB[7]])
                k.op("pool", lambda e: e.tensor_tensor(out=t1[:, 0:tw], in0=rg[:, 0:tw], in1=cosT[:, t0:t0 + tw], op=ALU.mult),
                     reads=[rgb, rp], writes=[t1b])

        def s3():
            if c == 4:
                k.op("dve", lambda e: e.tensor_tensor(out=out_ap, in0=rg[:, 0:tw], in1=rs_[:, 0:tw], op=ALU.mult),
                     reads=[rgb, rsb], writes=[out_buf])
                return
            k.op("dve", lambda e: e.tensor_tensor(out=t2[:, 0:tw], in0=ps[7][:, 0:tw], in1=sinT[:, t0:t0 + tw], op=ALU.mult),
                 reads=[PSB[7], rp], writes=[t2b])
            k.op("dve", lambda e: e.tensor_tensor(out=t1[:, 0:tw], in0=t1[:, 0:tw], in1=t2[:, 0:tw], op=ALU.add),
                 reads=[t1b, t2b], writes=[t1b])
            k.op("dve", lambda e: e.tensor_tensor(out=out_ap, in0=t1[:, 0:tw], in1=rs_[:, 0:tw], op=ALU.mult),
                 reads=[t1b, rsb], writes=[out_buf])

        s1()
        q = nr["q"]
        q.append((s2, s3))
        if len(q) > 1:
            p2, p3 = q.pop(0)
            p2()
            p3()

    def normrope_flush():
        q = nr["q"]
        while q:
            p2, p3 = q.pop(0)
            p2()
            p3()

    KNB = bf("kT_na")
    VNB = bf("v_na")
    QNB = [[bf("qn_%d_%d" % (pr, c)) for c in range(5)] for pr in range(4)]
    QGB = [[bf("qg_%d_%d" % (rr, c)) for c in range(5)] for rr in range(4)]
    KSB, VSB = bf("kst"), bf("vst")

    def phase_proj_kv(l):
        rp = bf("rope")
        k.dma("sp", d_rope, cosT[:], cos_in, writes=[rp])
        k.dma("sp", d_rope, sinT[:], sin_in, writes=[rp])
        k.seal(d_rope, [rp])
        k.op("pool", lambda e: e.memset(v_na[:, :, :].rearrange("p t (h c) -> p t h c", c=65)[:, :, :, 64:65], 1.0),
             writes=[VNB])
        k.op("pool", lambda e: e.memset(vst[:, :, :].rearrange("p t (h c) -> p t h c", c=65)[:, :, :, 64:65], 1.0),
             writes=[VSB])
        wl = w_in[l]
        sl_v, lb = wload([(full, w_kc(wl, 2048, 256))], [128, KC, 256], "gkv")
        for c in range(5):
            t0, tw = CH[c]
            pb = next_ps(0, 4)
            proj_fm(sl_v, lb, 0, c, pb)
            gqa_normrope(l, pb, c, kgT, kst[:, t0:t0 + tw], KSB, False)
        normrope_flush()
        for tile in range(18):
            c = tile // 4
            pb = next_ps(0, 4)
            for kc in range(KC):
                k.op("pe", lambda e, kc=kc, tile=tile, pb=pb: e.matmul(
                    ps[pb][:, 0:128], lhsT=hT[:, kc, tile * 128:(tile + 1) * 128], rhs=sl_v[:, kc, 128:256],
                    start=(kc == 0), stop=(kc == KC - 1)), reads=[lb, HB[c]], writes=[PSB[pb]])
            dst = vst[:, tile, :].rearrange("p (h c) -> p h c", c=65)[:, :, 0:64]
            src = ps[pb][:, 0:128].rearrange("p (h c) -> p h c", c=64)
            k.op("act", lambda e, dst=dst, src=src: e.activation(out=dst, in_=src, func=AF.Copy), reads=[PSB[pb]], writes=[VSB])
        for grp in range(2):
            sl_v, lb = wload([(full, w_kc(wl, 512 + grp * 256, 256))], [128, KC, 256], "nak")
            for cc in range(2):
                pr = grp * 2 + cc
                for c in range(5):
                    t0, tw = CH[c]
                    pb = next_ps(0, 4)
                    proj_fm(sl_v, lb, cc, c, pb)
                    ko = 256 + t0 if c < 4 else 2560
                    k.op("act", lambda e, pr=pr, pb=pb, ko=ko, tw=tw: e.activation(
                        out=kT_na[:, pr, ko:ko + tw], in_=ps[pb][:, 0:tw], func=AF.Copy), reads=[PSB[pb]], writes=[KNB])
        for grp in range(2):
            sl_v, lb = wload([(full, w_kc(wl, 1024 + grp * 256, 256))], [128, KC, 256], "nav")
            for tile in range(18):
                c = tile // 4
                pb = next_ps(0, 4)
                for kc in range(KC):
                    k.op("pe", lambda e, kc=kc, tile=tile, pb=pb: e.matmul(
                        ps[pb][:, 0:256], lhsT=hT[:, kc, tile * 128:(tile + 1) * 128], rhs=sl_v[:, kc, :],
                        start=(kc == 0), stop=(kc == KC - 1)), reads=[lb, HB[c]], writes=[PSB[pb]])
                ti = tile + 2 if tile < 16 else tile + 4
                dst = v_na[:, ti, :].rearrange("p (h c) -> p h c", c=65)[:, grp * 4:(grp + 1) * 4, 0:64]
                src = ps[pb][:, 0:256].rearrange("p (h c) -> p h c", c=64)
                k.op("act", lambda e, dst=dst, src=src: e.activation(out=dst, in_=src, func=AF.Copy),
                     reads=[PSB[pb]], writes=[VNB])

    def phase_proj_q(l, qch):
        wl = w_in[l]
        for grp in range(2):
            sl_v, lb = wload([(full, w_kc(wl, grp * 256, 256))], [128, KC, 256], "naq")
            for cc in range(2):
                pr = grp * 2 + cc
                for c in qch:
                    t0, tw = CH[c]
                    pb = next_ps(0, 4)
                    proj_fm(sl_v, lb, cc, c, pb)
                    k.op("act", lambda e, pr=pr, pb=pb, t0=t0, tw=tw: e.activation(
                        out=qn[:, pr, t0:t0 + tw], in_=ps[pb][:, 0:tw], func=AF.Copy, scale=0.125),
                        reads=[PSB[pb]], writes=[QNB[pr][c]])
        for grp in range(2):
            srcs = []
            for g in range(2):
                for r in range(2):
                    c0 = 1536 + g * 256 + (grp * 2 + r) * 64
                    src = wl[:, c0:c0 + 64].rearrange("(kc p) d -> p kc d", p=128)
                    srcs.append((lambda v, g=g, r=r: v.rearrange("p kc (r g d) -> p kc r g d", g=2, d=64)[:, :, r, g, :], src))
            sl_v, lb = wload(srcs, [128, KC, 256], "gq")
            for cc in range(2):
                rr = grp * 2 + cc
                for c in qch:
                    t0, tw = CH[c]
                    pb = next_ps(0, 4)
                    proj_fm(sl_v, lb, cc, c, pb)
                    gqa_normrope(l, pb, c, qgT, qg[:, rr, t0:t0 + tw], QGB[rr][c], True)
        normrope_flush()

    def phase_exchange():
        xb = [bf("xin%d" % p) for p in range(4)]
        xi = [t.ap() for t in xin]
        k.dma("sp", d_xin, xi[0][:, 0:2048], kst[:, 0:2048], reads=[KSB], writes=[xb[0]])
        k.dma("sp", d_xin, xi[1][:, :], vst[:, 0:16, :].rearrange("p t c -> p (t c)"), reads=[VSB], writes=[xb[1]])
        kb_v = xi[2][:, 0:2048].rearrange("p (tb pr c) -> p tb pr c", tb=2, pr=4)
        k.dma("sp", d_xin, kb_v[:, 0, :, :], kT_na[:, :, 256:512], reads=[KNB], writes=[xb[2]])
        k.dma("sp", d_xin, kb_v[:, 1, :, :], kT_na[:, :, 2048:2304], reads=[KNB], writes=[xb[2]])
        k.dma("sp", d_xin, xi[3][:, 0:1040], v_na[:, 2:4, :].rearrange("p t c -> p (t c)"), reads=[VNB], writes=[xb[3]])
        k.dma("sp", d_xin, xi[3][:, 1040:2080], v_na[:, 16:18, :].rearrange("p t c -> p (t c)"), reads=[VNB], writes=[xb[3]])
        k.seal(d_xin, xb)
        e = k.E["pool"]
        for p in (2, 3, 0, 1):
            ob = bf("xout%d" % p)
            k._wait(e, [xb[p]], [ob])
            ins = nc.gpsimd.collective_compute("AllGather", ALU.bypass, replica_groups=[[0, 1, 2, 3], [4, 5, 6, 7]],
                                               ins=[xin[p].ap().opt()], outs=[xout[p].ap().opt()])
            cc_cnt[0] += 1
            ins.then_inc(ccsem, 1)
            k.n_ins += 1
            ob.w["cc"] = (ccsem, cc_cnt[0])
            xb[p].r["cc"] = (ccsem, cc_cnt[0])

    def phase_halo():
        for r in range(4):
            i = r % 2
            sk = fat("stgk%d" % i, [128, XROW], BF16, i * 8320)
            sv = fat("stgv%d" % i, [128, XROW], BF16, i * 8320 + 4160)
            sb = bf("stg%d" % i)
            k.dma("sp", d_halo[i], sk[:, 0:2048], xout[2].ap()[r * 128:(r + 1) * 128, :], reads=[bf("xout2")], writes=[sb])
            k.dma("sp", d_halo[i], sv[:], xout[3].ap()[r * 128:(r + 1) * 128, :], reads=[bf("xout3")], writes=[sb])
            k.seal(d_halo[i], [sb])
            skv = sk[:, 0:2048].rearrange("p (tb pr c) -> p tb pr c", tb=2, pr=4)
            pieces = [
                (kT_na[:, :, 0:256], skv[:, 1, :, :], r, KNB),
                (kT_na[:, :, 2304:2560], skv[:, 0, :, :], 4 + r, KNB),
                (v_na[:, 0:2, :].rearrange("p t c -> p (t c)"), sv[:, 1040:2080], r, VNB),
                (v_na[:, 18:20, :].rearrange("p t c -> p (t c)"), sv[:, 0:1040], 4 + r, VNB),
            ]
            for dst, src, si, db in pieces:
                if r == 0:
                    k.op("dve", lambda e, dst=dst, src=src, si=si: e.tensor_scalar(
                        out=dst, in0=src, scalar1=sel[:, si:si + 1], scalar2=None, op0=ALU.mult),
                        reads=[sb, cb], writes=[db])
                else:
                    k.op("dve", lambda e, dst=dst, src=src, si=si: e.scalar_tensor_tensor(
                        out=dst, in0=src, scalar=sel[:, si:si + 1], in1=dst, op0=ALU.mult, op1=ALU.add),
                        reads=[sb, cb, db], writes=[db])

    att = {"p": 0, "o": 0, "s": 0, "t": 0, "q": 0, "pending": None, "hook": None, "nsg": 2, "look": 1, "obanks": (4, 5), "bcbank": None, "fdelay": 8}

    def att_flush():
        if att["pending"] is not None:
            f = att["pending"]
            att["pending"] = None
            f()
    QP = [fat("qpad%d" % i, [128, 512], BF16, 12288 + i * 1024) for i in range(4)]
    QPB = [bf("qpad%d" % i) for i in range(4)]

    RSP = [fat("rsp%d" % i, [128, 2, 512], BF16, 16384 + i * 2048) for i in range(2)]

    def att_init():
        for i in range(4):
            k.op("pool", lambda e, i=i: e.memset(QP[i][:], 0.0), writes=[QPB[i]])
        for i in range(2):
            k.op("pool", lambda e, i=i: e.memset(RSP[i][:], 0.0), writes=[bf("rsp%d" % i)])

    def attention_phase(calls):
        LOOK = att["look"]
        pt = [fat("pT%d" % i, [128, 2, 512], BF16, i * 2048) for i in range(4)]
        otsb = [fat("otsb%d" % i, [128, 512], F32, 8192 + i * 2048) for i in range(2)]
        rsp = RSP
        rsb = [bf("rsp0"), bf("rsp1")]
        items = []
        for ci, c in enumerate(calls):
            assert len(c["qk"]) % 2 == 0
            for j in range(len(c["qk"]) // 2):
                items.append((ci, j))
        info = {}
        pend = []

        def setup_q(ci):
            c = calls[ci]
            hh = c["part0"] // 64
            qi = hh * 2 + att["q"] % 2
            att["q"] += 1
            qp = QP[qi]
            p0, qw = c["part0"], c["qw"]
            k.op("pool", lambda e: e.tensor_copy(out=qp[p0:p0 + 64, 0:qw], in_=c["q_ap"]), reads=[c["buf"]], writes=[QPB[qi]])
            info[ci] = dict(qi=qi)

        def setup_o(ci):
            oi = att["o"] % 2
            att["o"] += 1
            info[ci].update(oi=oi, ob=att["obanks"][oi], sg={})

        def flush(now, upto_ci):
            keep = []
            for due, pci, f in pend:
                if due <= now or pci <= upto_ci:
                    f()
                else:
                    keep.append((due, pci, f))
            pend[:] = keep

        if calls:
            setup_q(0)
        for it in range(len(items) + LOOK):
            if it < len(items):
                ci, j = items[it]
                c = calls[ci]
                qw = c["qw"]
                if j == 0:
                    if c.get("pre") is not None:
                        c["pre"]()
                    setup_o(ci)
                    if ci + 1 < len(calls):
                        setup_q(ci + 1)
                inf = info[ci]
                qp = QP[inf["qi"]]
                sg = att["s"] % att["nsg"]
                att["s"] += 1
                inf["sg"][j] = sg
                for half in range(2):
                    d = c["qk"][2 * j + half]
                    bank = 2 * sg + half
                    extra = []
                    if d.get("pen") is not None:
                        extra.append((d["pen"][0], d["pen"][1], [bf("onehot"), bf("pen")]))
                    if d.get("bias") is not None:
                        extra.append((ident_bf[:, :], d["bias"], [bf("onehot"), d["biasbuf"]]))
                    k.op("pe", lambda e, d=d, bank=bank, ne=len(extra), qp=qp, qw=qw: e.matmul(
                        ps[bank][:, 0:qw], lhsT=d["kT"], rhs=qp[:, 0:qw], start=True, stop=(ne == 0)),
                        reads=[d["kbuf"], QPB[inf["qi"]]], writes=[PSB[2 * sg]])
                    for xi_, (l_ap, r_ap, rd) in enumerate(extra):
                        k.op("pe", lambda e, bank=bank, l_ap=l_ap, r_ap=r_ap, qw=qw, lastx=(xi_ == len(extra) - 1): e.matmul(
                            ps[bank][:, 0:qw], lhsT=l_ap, rhs=r_ap, start=False, stop=lastx),
                            reads=rd, writes=[PSB[2 * sg]])
                if j == 0 and att["hook"] is not None:
                    att["hook"]()
            flush(it, -1)
            jt = it - LOOK
            if jt >= 0:
                ci, j = items[jt]
                c = calls[ci]
                qw = c["qw"]
                inf = info[ci]
                sg = inf["sg"][j]
                n = len(c["qk"])
                ob_i, oi = inf["ob"], inf["oi"]
                if j == 0:
                    flush(-1, ci - 2)
                pi = att["p"] % 4
                att["p"] += 1
                pbuf = bf("pT%d" % pi)
                s_ap = psall[:, 2 * sg * 512:(2 * sg + 2) * 512].rearrange("p (a b) -> p a b", a=2)[:, :, 0:qw]
                k.op("act", lambda e, s_ap=s_ap, pi=pi, qw=qw: e.activation(out=pt[pi][:, :, 0:qw], in_=s_ap, func=AF.Exp),
                     reads=[PSB[2 * sg]], writes=[pbuf])
                for half in range(2):
                    d = c["qk"][2 * j + half]
                    jj = 2 * j + half
                    k.op("pe", lambda e, d=d, pi=pi, half=half, jj=jj, ob_i=ob_i, qw=qw, n=n: e.matmul(
                        ps[ob_i][0:65, 0:qw], lhsT=d["v"], rhs=pt[pi][:, half, 0:qw], start=(jj == 0), stop=(jj == n - 1)),
                        reads=[d["vbuf"], pbuf], writes=[PSB[ob_i]])
                if 2 * j + 2 == n:
                    def fin_a(c=c, ob_i=ob_i, oi=oi, qw=qw):
                        osb = bf("otsb%d" % oi)
                        k.op("dve", lambda e: e.tensor_copy(out=otsb[oi][0:65, 0:qw], in_=ps[ob_i][0:65, 0:qw]),
                             reads=[PSB[ob_i]], writes=[osb])
                        k.op("dve", lambda e: e.reciprocal(out=otsb[oi][64:65, 0:qw], in_=otsb[oi][64:65, 0:qw]),
                             reads=[osb], writes=[osb])
                        k.op("pool", lambda e: e.tensor_copy(out=rsp[oi][64:65, 0, 0:qw], in_=otsb[oi][64:65, 0:qw]),
                             reads=[osb], writes=[rsb[oi]])
                        k.op("pool", lambda e: e.tensor_tensor(out=rsp[oi][64:65, 1, 0:qw], in0=otsb[oi][64:65, 0:qw],
                                                               in1=rsp[oi][64:65, 0, 0:qw], op=ALU.subtract),
                             reads=[osb, rsb[oi]], writes=[rsb[oi]])

                    def fin_b(c=c, ob_i=ob_i, oi=oi, qw=qw):
                        osb = bf("otsb%d" % oi)
                        out_ap, obuf = c["out_ap"], c["buf"]
                        bcb = att["bcbank"] if att["bcbank"] is not None else ob_i
                        for t_ in range(2):
                            k.op("pe", lambda e, t_=t_: e.matmul(ps[bcb][:, 0:qw], lhsT=sel64_bf[:, :],
                                                                rhs=rsp[oi][:, t_, 0:qw], start=(t_ == 0), stop=(t_ == 1)),
                                 reads=[rsb[oi], cb], writes=[PSB[bcb]])
                        k.op("dve", lambda e: e.tensor_tensor(out=out_ap, in0=otsb[oi][0:64, 0:qw], in1=ps[bcb][0:64, 0:qw],
                                                              op=ALU.mult), reads=[osb, PSB[bcb], obuf], writes=[obuf])
                    pend.append((it + 1, ci, fin_a))
                    pend.append((it + 1 + att["fdelay"], ci, fin_b))
        flush(10 ** 9, 10 ** 9)

    na_state = {}

    def na_load_tab(l, pr):
        i = pr % 2
        tbs = bf("stage%d" % i)
        tbb = bf("slot%d" % i)
        k.dma("sp", d_tb[i], tbias[i][:].rearrange("p a b c -> p (a b c)"), tb_in[l, pr], writes=[tbs])
        k.op("dve", lambda e, i=i: e.tensor_copy(out=tb16[i][:].rearrange("p a b c -> p (a b c)"),
                                                 in_=tbias[i][:].rearrange("p a b c -> p (a b c)")),
             reads=[tbs], writes=[tbb])

    def phase_na(l, qch, mod_next):
        ohv = onehot[:, :].rearrange("m (kt n) -> m kt n", kt=8)
        penv = pen[:, :].rearrange("m (u n) -> m u n", u=4)
        att_init()
        att.update(nsg=2, look=1, obanks=(4, 5), bcbank=6, fdelay=5)
        st = {"g": 0}

        def hook():
            g = st["g"]
            if mod_next is None or g >= 24:
                return
            if g == 0:
                mod_dma(mod_next, 0)
            if g + 1 < 24:
                mod_dma(mod_next, g + 1)
            mod_mm(mod_next, g)
            st["g"] = g + 1
        att["hook"] = hook
        na_state["st"], na_state["hook"] = st, hook
        mod_use_alt(True)
        def load_tab(pr):
            na_load_tab(l, pr)

        calls = []
        for pr in range(4):
            i = pr % 2
            tbb = bf("slot%d" % i)
            first = True
            for hh in range(2):
                p0 = hh * 64
                h = 2 * pr + hh
                for u in qch:
                    t0, qw = CH[u]
                    lst = []
                    if u < 4:
                        for kt in range(8):
                            kc0 = 512 * u + 128 * kt
                            lst.append(dict(
                                kT=kT_na[:, pr, kc0:kc0 + 128], kbuf=KNB,
                                v=v_na[:, 4 * u + kt, h * 65:(h + 1) * 65], vbuf=VNB,
                                bias=tb16[i][:, hh, 14 - 2 * kt:22 - 2 * kt, :].rearrange("p a b -> p (a b)"), biasbuf=tbb,
                                pen=(ohv[:, kt, :], penv[:, u, :])))
                    for m in range(2):
                        lst.append(dict(kT=kT_na[:, pr, 2560 + 128 * m:2560 + 128 * (m + 1)], kbuf=KNB,
                                        v=v_na[:, 20 + m, h * 65:(h + 1) * 65], vbuf=VNB))
                    pre = None
                    if first:
                        first = False
                        if pr == 0:
                            pre = None
                        elif pr + 1 < 4:
                            pre = lambda pr=pr: load_tab(pr + 1)
                    calls.append(dict(qk=lst, q_ap=qn[p0:p0 + 64, pr, t0:t0 + qw], qw=qw,
                                      out_ap=qn[p0:p0 + 64, pr, t0:t0 + qw], buf=QNB[pr][u], part0=p0, pre=pre))
        attention_phase(calls)

    KGB, VGB = bf("kT_g"), bf("v_g")

    def phase_gqa_load():
        for r in range(4):
            k.dma("sp", d_kv, kT_g[:, r * 2048:(r + 1) * 2048], xout[0].ap()[r * 128:(r + 1) * 128, 0:2048],
                  reads=[bf("xout0")], writes=[KGB])
            k.dma("sp", d_kv, v_g[:, r * 16:(r + 1) * 16, :].rearrange("p t c -> p (t c)"),
                  xout[1].ap()[r * 128:(r + 1) * 128, :], reads=[bf("xout1")], writes=[VGB])
        k.seal(d_kv, [KGB, VGB])
        k.op("pool", lambda e: e.tensor_copy(out=kT_g[:, 8192:8448], in_=kst[:, 2048:2304]), reads=[KSB], writes=[KGB])
        k.op("pool", lambda e: e.tensor_copy(out=v_g[:, 64:66, :], in_=vst[:, 16:18, :]), reads=[VSB], writes=[VGB])

    def phase_gqa(l, qch, mod_next):
        att_init()
        att.update(nsg=3, look=2, obanks=(6, 7), bcbank=None, fdelay=10)
        calls = []
        for g in range(2):
            p0 = g * 64
            for rr in range(4):
                for u in qch:
                    t0, qw = CH[u]
                    kts = range(66) if u < 4 else range(64, 66)
                    lst = [dict(kT=kT_g[:, kt * 128:(kt + 1) * 128], kbuf=KGB,
                                v=v_g[:, kt, g * 65:(g + 1) * 65], vbuf=VGB) for kt in kts]
                    calls.append(dict(qk=lst, q_ap=qg[p0:p0 + 64, rr, t0:t0 + qw], qw=qw,
                                      out_ap=qg[p0:p0 + 64, rr, t0:t0 + qw], buf=QGB[rr][u], part0=p0))
        attention_phase(calls)

    def wload_m(srcs, shape):
        return wload(srcs, shape, "m")

    def phase_merge(l, blks):
        mb = cur["mb"]
        modT = cur["modT"]
        merged = fat("merged", [128, KC, 1280], BF16, 0)
        sga = at("sga", [128, 512], F32, W + 8192)
        zb = at("zb", [128, 512], F32, W + 19456)
        sgb_, zbb = bf("sga"), bf("zb")
        wl = w_in[l]
        for blk in blks:
            bt0 = CH[blk[0]][0]
            MB = {c: bf("merged_%d" % c) for c in blk}
            for fg in range(4):
                sl_ga, lb_ga = wload([(full, w_kc(wl, 2304 + fg * 256, 256))], [128, KC, 256], "ga")
                sl_pa, lb_pa = wload([(full, w_pa[l][:, fg * 256:(fg + 1) * 256].rearrange("(pr p) n -> p pr n", p=128))],
                                     [128, 4, 256], "pa")
                for fc2 in range(2):
                    fc = fg * 2 + fc2
                    for c in blk:
                        t0, tw = CH[c]
                        b1, b2 = next_ps(0, 8), next_ps(0, 8)
                        for kc in range(KC):
                            k.op("pe", lambda e, kc=kc, b1=b1, fc2=fc2, t0=t0, tw=tw: e.matmul(
                                ps[b1][:, 0:tw], lhsT=sl_ga[:, kc, fc2 * 128:(fc2 + 1) * 128], rhs=hT[:, kc, t0:t0 + tw],
                                start=(kc == 0), stop=(kc == KC - 1)), reads=[lb_ga, HB[c]], writes=[PSB[b1]])
                        for pr in range(4):
                            k.op("pe", lambda e, pr=pr, b2=b2, fc2=fc2, t0=t0, tw=tw: e.matmul(
                                ps[b2][:, 0:tw], lhsT=sl_pa[:, pr, fc2 * 128:(fc2 + 1) * 128], rhs=qn[:, pr, t0:t0 + tw],
                                start=(pr == 0), stop=(pr == 3)), reads=[lb_pa, QNB[pr][c]], writes=[PSB[b2]])
                        k.op("act", lambda e, b1=b1, tw=tw: e.activation(out=sga[:, 0:tw], in_=ps[b1][:, 0:tw], func=AF.Sigmoid),
                             reads=[PSB[b1]], writes=[sgb_])
                        k.op("dve", lambda e, b2=b2, fc=fc, t0=t0, tw=tw: e.tensor_tensor(
                            out=merged[:, fc, t0 - bt0:t0 - bt0 + tw], in0=sga[:, 0:tw], in1=ps[b2][:, 0:tw], op=ALU.mult),
                            reads=[sgb_, PSB[b2]], writes=[MB[c]])
                sl_gb, lb_gb = wload([(full, w_kc(wl, 3328 + fg * 256, 256))], [128, KC, 256], "gb")
                sl_pb, lb_pb = wload([(lambda v, g=g: v[64 * g:64 * g + 64, :, :],
                                       w_pb[l][g * 256:(g + 1) * 256, fg * 256:(fg + 1) * 256].rearrange("(rr d) n -> d rr n", d=64))
                                      for g in range(2)], [128, 4, 256], "pb")
                for fc2 in range(2):
                    fc = fg * 2 + fc2
                    for c in blk:
                        t0, tw = CH[c]
                        b1, b2 = next_ps(0, 8), next_ps(0, 8)
                        for kc in range(KC):
                            k.op("pe", lambda e, kc=kc, b1=b1, fc2=fc2, t0=t0, tw=tw: e.matmul(
                                ps[b1][:, 0:tw], lhsT=sl_gb[:, kc, fc2 * 128:(fc2 + 1) * 128], rhs=hT[:, kc, t0:t0 + tw],
                                start=(kc == 0), stop=(kc == KC - 1)), reads=[lb_gb, HB[c]], writes=[PSB[b1]])
                        for rr in range(4):
                            k.op("pe", lambda e, rr=rr, b2=b2, fc2=fc2, t0=t0, tw=tw: e.matmul(
                                ps[b2][:, 0:tw], lhsT=sl_pb[:, rr, fc2 * 128:(fc2 + 1) * 128], rhs=qg[:, rr, t0:t0 + tw],
                                start=(rr == 0), stop=(rr == 3)), reads=[lb_pb, QGB[rr][c]], writes=[PSB[b2]])
                        k.op("act", lambda e, b1=b1, tw=tw: e.activation(out=sga[:, 0:tw], in_=ps[b1][:, 0:tw], func=AF.Sigmoid),
                             reads=[PSB[b1]], writes=[sgb_])
                        k.op("dve", lambda e, b2=b2, tw=tw: e.tensor_tensor(
                            out=zb[:, 0:tw], in0=sga[:, 0:tw], in1=ps[b2][:, 0:tw], op=ALU.mult),
                            reads=[sgb_, PSB[b2]], writes=[zbb])
                        k.op("dve", lambda e, fc=fc, t0=t0, tw=tw: e.tensor_tensor(
                            out=merged[:, fc, t0 - bt0:t0 - bt0 + tw], in0=merged[:, fc, t0 - bt0:t0 - bt0 + tw],
                            in1=zb[:, 0:tw], op=ALU.add), reads=[zbb, MB[c]], writes=[MB[c]])
            for og in range(4):
                sl_o, lb_o = wload([(full, w_kc(w_o[l], og * 256, 256))], [128, KC, 256], "wo")
                for oc2 in range(2):
                    oc = og * 2 + oc2
                    for c in blk:
                        t0, tw = CH[c]
                        col = colsel(c)
                        b1 = next_ps(0, 8)
                        for fc in range(KC):
                            k.op("pe", lambda e, fc=fc, b1=b1, oc2=oc2, t0=t0, tw=tw: e.matmul(
                                ps[b1][:, 0:tw], lhsT=sl_o[:, fc, oc2 * 128:(oc2 + 1) * 128],
                                rhs=merged[:, fc, t0 - bt0:t0 - bt0 + tw], start=(fc == 0), stop=(fc == KC - 1)),
                                reads=[lb_o, MB[c]], writes=[PSB[b1]])
                        k.op("dve", lambda e, b1=b1, oc=oc, t0=t0, tw=tw, col=col: e.scalar_tensor_tensor(
                            out=xT[:, oc, t0:t0 + tw], in0=ps[b1][:, 0:tw], scalar=modT[:, 16 + oc, col:col + 1],
                            in1=xT[:, oc, t0:t0 + tw], op0=ALU.mult, op1=ALU.add),
                            reads=[PSB[b1], mb, XB[c]], writes=[XB[c]])

    def phase_ffn(l, blks):
        mb = cur["mb"]
        modT = cur["modT"]
        h2 = at("h2T", [128, KC, 1280], BF16, HT)
        gT = at("gT", [128, FC, 1280], BF16, HT + 20480)
        sa = [at("sa0", [128, 512], F32, FR + 3072 + 10240), at("sa1", [128, 512], F32, FR + 3072 + 12288)]
        sab = [bf("n2tmp0"), bf("n2tmp1")]
        for blk in blks:
            bt0 = CH[blk[0]][0]
            H2B = {c: bf("h2_%d" % c) for c in blk}
            GB = {c: bf("g_%d" % c) for c in blk}
            for c in blk:
                norm_chunk_ffn(c, h2, CH[c][0] - bt0, H2B[c])
            for fg in range(FC // 2):
                sl_a, lb_a = wload([(full, w_kc(w_f1[l], fg * 256, 256))], [128, KC, 256], "fa")
                sl_u, lb_u = wload([(full, w_kc(w_f1[l], DFF + fg * 256, 256))], [128, KC, 256], "fu")
                for fc2 in range(2):
                    fc = fg * 2 + fc2
                    for c in blk:
                        t0, tw = CH[c]
                        o0 = t0 - bt0
                        b1, b2 = next_ps(0, 8), next_ps(0, 8)
                        for kc in range(KC):
                            k.op("pe", lambda e, kc=kc, b1=b1, fc2=fc2, o0=o0, tw=tw: e.matmul(
                                ps[b1][:, 0:tw], lhsT=sl_a[:, kc, fc2 * 128:(fc2 + 1) * 128], rhs=h2[:, kc, o0:o0 + tw],
                                start=(kc == 0), stop=(kc == KC - 1)), reads=[lb_a, H2B[c]], writes=[PSB[b1]])
                        for kc in range(KC):
                            k.op("pe", lambda e, kc=kc, b2=b2, fc2=fc2, o0=o0, tw=tw: e.matmul(
                                ps[b2][:, 0:tw], lhsT=sl_u[:, kc, fc2 * 128:(fc2 + 1) * 128], rhs=h2[:, kc, o0:o0 + tw],
                                start=(kc == 0), stop=(kc == KC - 1)), reads=[lb_u, H2B[c]], writes=[PSB[b2]])
                        si = fc % 2
                        k.op("act", lambda e, b1=b1, si=si, tw=tw: e.activation(out=sa[si][:, 0:tw], in_=ps[b1][:, 0:tw], func=AF.Silu),
                             reads=[PSB[b1]], writes=[sab[si]])
                        k.op("dve", lambda e, b2=b2, si=si, fc=fc, o0=o0, tw=tw: e.tensor_tensor(
                            out=gT[:, fc, o0:o0 + tw], in0=sa[si][:, 0:tw], in1=ps[b2][:, 0:tw], op=ALU.mult),
                            reads=[sab[si], PSB[b2]], writes=[GB[c]])
            for oc in range(KC):
                sl_o, lb_o = wload([(full, w_f2[l][:, oc * 128:(oc + 1) * 128].rearrange("(fc p) n -> p fc n", p=128))],
                                   [128, FC, 128], "f2")
                for c in blk:
                    t0, tw = CH[c]
                    o0 = t0 - bt0
                    col = colsel(c)
                    b1 = next_ps(0, 8)
                    for fc in range(FC):
                        k.op("pe", lambda e, fc=fc, b1=b1, o0=o0, tw=tw: e.matmul(
                            ps[b1][:, 0:tw], lhsT=sl_o[:, fc, :], rhs=gT[:, fc, o0:o0 + tw],
                            start=(fc == 0), stop=(fc == FC - 1)), reads=[lb_o, GB[c]], writes=[PSB[b1]])
                    k.op("dve", lambda e, b1=b1, oc=oc, t0=t0, tw=tw, col=col: e.scalar_tensor_tensor(
                        out=xT[:, oc, t0:t0 + tw], in0=ps[b1][:, 0:tw], scalar=modT[:, 40 + oc, col:col + 1],
                        in1=xT[:, oc, t0:t0 + tw], op0=ALU.mult, op1=ALU.add),
                        reads=[PSB[b1], mb, XB[c]], writes=[XB[c]])

    def norm_chunk_ffn(c, h2, off, buf):
        t0, tw = CH[c]
        col = colsel(c)
        mb = cur["mb"]
        modT, A2 = cur["modT"], cur["A2"]
        o = FR + 3072
        sq = at("sq2", [128, KC, 512], BF16, o)
        rstd = at("rstd2", [128, 512], F32, o + 8192)
        tmp = [at("n2tmp0", [128, 512], F32, o + 10240), at("n2tmp1", [128, 512], F32, o + 12288)]
        sqb, rb = bf("sq2"), bf("rstd2")
        tb_ = [bf("n2tmp0"), bf("n2tmp1")]
        k.op("act", lambda e: e.activation(out=sq[:, :, 0:tw], in_=xT[:, :, t0:t0 + tw], func=AF.Square),
             reads=[XB[c]], writes=[sqb])
        pb = 7
        for kc in range(KC):
            k.op("pe", lambda e, kc=kc: e.matmul(ps[pb][:, 0:tw], lhsT=ones_bf[:], rhs=sq[:, kc, 0:tw],
                                                 start=(kc == 0), stop=(kc == KC - 1)), reads=[sqb, cb], writes=[PSB[pb]])
        k.op("act", lambda e: e.activation(out=rstd[:, 0:tw], in_=ps[pb][:, 0:tw], func=AF.Sqrt, scale=1.0 / D,
                                           bias=eps_t[:, 0:1]), reads=[PSB[pb], cb], writes=[rb])
        k.op("dve", lambda e: e.reciprocal(out=rstd[:, 0:tw], in_=rstd[:, 0:tw]), reads=[rb], writes=[rb])
        for kc in range(KC):
            i = kc % 2
            k.op("dve", lambda e, kc=kc, i=i: e.scalar_tensor_tensor(
                out=tmp[i][:, 0:tw], in0=xT[:, kc, t0:t0 + tw], scalar=A2[:, kc, col:col + 1], in1=rstd[:, 0:tw],
                op0=ALU.mult, op1=ALU.mult), reads=[XB[c], mb, rb], writes=[tb_[i]])
            k.op("act", lambda e, kc=kc, i=i: e.activation(
                out=h2[:, kc, off:off + tw], in_=tmp[i][:, 0:tw], func=AF.Identity,
                bias=modT[:, 24 + kc, col:col + 1], scale=1.0), reads=[tb_[i], mb], writes=[buf])

    def phase_final():
        o = FR
        sq = at("sqf", [128, KC, 512], BF16, o)
        rstd = at("rstdf", [128, 512], F32, o + 8192)
        ob = [at("of0", [128, 512], F32, o + 10240), at("of1", [128, 512], F32, o + 12288)]
        sqb, rb = bf("sqf"), bf("rstdf")
        obb = [bf("of0"), bf("of1")]
        yb = bf("yout")
        yv = y_out.rearrange("(kc p) t -> p kc t", p=128)
        n = 0
        for c in range(4):
            t0, tw = CH[c]
            k.op("act", lambda e: e.activation(out=sq[:, :, 0:tw], in_=xT[:, :, t0:t0 + tw], func=AF.Square),
                 reads=[XB[c]], writes=[sqb])
            pb = 7
            for kc in range(KC):
                k.op("pe", lambda e, kc=kc: e.matmul(ps[pb][:, 0:tw], lhsT=ones_bf[:], rhs=sq[:, kc, 0:tw],
                                                     start=(kc == 0), stop=(kc == KC - 1)), reads=[sqb, cb], writes=[PSB[pb]])
            k.op("act", lambda e: e.activation(out=rstd[:, 0:tw], in_=ps[pb][:, 0:tw], func=AF.Sqrt, scale=1.0 / D,
                                               bias=eps_t[:, 0:1]), reads=[PSB[pb], cb], writes=[rb])
            k.op("dve", lambda e: e.reciprocal(out=rstd[:, 0:tw], in_=rstd[:, 0:tw]), reads=[rb], writes=[rb])
            for kc in range(KC):
                i = n % 2
                n += 1
                k.op("dve", lambda e, kc=kc, i=i: e.scalar_tensor_tensor(
                    out=ob[i][:, 0:tw], in0=xT[:, kc, t0:t0 + tw], scalar=fnT[:, kc:kc + 1], in1=rstd[:, 0:tw],
                    op0=ALU.mult, op1=ALU.mult), reads=[XB[c], cb, rb], writes=[obb[i]])
                k.dma("sp", d_out, yv[:, kc, t0:t0 + tw], ob[i][:, 0:tw], reads=[obb[i]], writes=[yb])
        k._wait(k.E["sp"], [yb], ())

    def dump(name, t_ap, npart, n):
        o = nc.dram_tensor("dbg_" + name, [npart, n], F32, kind="ExternalOutput").ap()
        dbg_names.append("dbg_" + name)
        db = bf("dbg_" + name)
        k.barrier()
        k.dma("pool", d_out, o, t_ap, writes=[db])
        k._wait(k.E["pool"], [db], ())

    def dbg_at(name, l):
        return debug is not None and debug[0] == name and debug[1] == l

    done = False
    fl = lambda t: t[:].rearrange("p a b -> p (a b)")
    mod_use_alt(False)
    phase_mod(0)
    for l in range(n_layers):
        last = (l == DEPTH - 1)
        qch = [0, 1, 2, 3] if last else [0, 1, 2, 3, 4]
        blks = [[0, 1], [2, 3]] if last else BLK
        set_layer(l)
        phase_norm1(l)
        if dbg_at("h", l):
            dump("h", fl(hT), 128, 8 * TT); done = True; break
        phase_spill()
        k.barrier()
        phase_proj_kv(l)
        phase_exchange()
        phase_proj_q(l, qch)
        k.barrier()
        if dbg_at("proj", l):
            dump("qn", fl(qn), 128, 4 * TT); dump("qg", fl(qg), 128, 4 * TT); dump("kn", fl(kT_na), 128, 4 * 2816)
            dump("vn", fl(v_na), 128, 22 * 520); dump("kst", kst[:], 128, TT); dump("vst", fl(vst), 128, 18 * 130)
            done = True; break
        na_load_tab(l, 0)
        na_load_tab(l, 1)
        phase_halo()
        k.barrier()
        if dbg_at("halo", l):
            dump("kn", fl(kT_na), 128, 4 * 2816); dump("vn", fl(v_na), 128, 22 * 520); done = True; break
        mod_next = l + 1 if l + 1 < n_layers else None
        phase_na(l, qch, mod_next)
        att["hook"] = None
        if mod_next is not None:
            while na_state["st"]["g"] < 24:
                na_state["hook"]()
            mod_finish(mod_next)
        k.barrier()
        if dbg_at("na", l):
            dump("yn", fl(qn), 128, 4 * TT); done = True; break
        phase_gqa_load()
        phase_gqa(l, qch, mod_next)
        k.barrier()
        if dbg_at("gqa", l):
            dump("yn", fl(qn), 128, 4 * TT); dump("yg", fl(qg), 128, 4 * TT); done = True; break
        phase_restore()
        phase_merge(l, blks)
        k.barrier()
        if dbg_at("merge", l):
            dump("x", fl(xT), 128, 8 * TT); done = True; break
        phase_ffn(l, blks)
        k.barrier()
        if dbg_at("ffn", l):
            dump("x", fl(xT), 128, 8 * TT); done = True; break
    if not done:
        phase_final()
    print("program: n_ins=%d n_wait=%d" % (k.n_ins, k.n_wait))
    nc._dbg_names = dbg_names
    return nc


def _rope_tables(n_tokens):
    t = np.arange(n_tokens)
    row = (t // GRID_W).astype(np.float32)
    col = (t % GRID_W).astype(np.float32)
    half = HD // 2
    inv = (np.float32(10000.0) ** (-np.arange(0, half, 2, dtype=np.float32) / np.float32(half))).astype(np.float32)
    ang = np.concatenate([row[:, None] * inv, col[:, None] * inv], axis=-1).astype(np.float32)
    return np.cos(ang).astype(np.float32), np.sin(ang).astype(np.float32)


def _bias_tables(na_rpb):
    p = np.arange(128)
    ck = p % 64
    sub = p // 64
    i = np.arange(22)
    cq = np.arange(64)
    dr = (17 + sub)[:, None] - i[None, :]
    dr_ok = (dr >= 0) & (dr <= 14)
    drc = np.clip(dr, 0, 14)
    dc = np.clip(ck[:, None] - cq[None, :], -15, 15) + 15
    c_start = np.clip(cq - 8, 0, GRID_W - 16)
    inwin = (ck[:, None] >= c_start[None, :]) & (ck[:, None] < c_start[None, :] + 16)
    out = np.zeros((DEPTH, 4, 128, 2, 22, 64), np.float32)
    for pr in range(4):
        for hh in range(2):
            h = 2 * pr + hh
            g = na_rpb[:, h][:, drc[:, :, None], dc[:, None, :]]
            g = np.where(dr_ok[None, :, :, None], g, np.float32(0.0))
            g = np.where(inwin[None, :, None, :], g, np.float32(NEG))
            out[:, pr, :, hh] = g
    return out.reshape(DEPTH, 4, 128, 2 * 22 * 64)


def _pen_table(j):
    rows = 128
    R0 = 32 * j
    pen = np.zeros((128, 4, 512), np.float32)
    pen[0:16] = NEG
    for u in range(4):
        for rq in range(8):
            R = R0 + 8 * u + rq
            rs = min(max(R - 4, 0), rows - 8)
            for m in range(16):
                gr = R0 + 8 * u - 4 + m
                if rs <= gr < rs + 8:
                    pen[m, u, rq * 64:(rq + 1) * 64] = 0.0
    return pen.reshape(128, 4 * 512)


def _consts():
    perm = np.zeros((128, 128), np.float32)
    for i in range(64):
        perm[2 * i + 1, 2 * i] = -1.0
        perm[2 * i, 2 * i + 1] = 1.0
    onehot = np.zeros((128, 8, 128), np.float32)
    for kt in range(8):
        for key in range(128):
            onehot[2 * kt + key // 64, kt, key] = 1.0
    return perm, onehot.reshape(128, 8 * 128)


_CACHE = {}
_N_LAYERS = DEPTH
_DEBUG = None


def kernel(x, c, ctx, c_ctx, w_mod, b_mod, norm1, w_in, na_rpb, q_gain, k_gain, w_pa, w_pb, w_o,
           norm2, w_ffn_in, w_ffn_out, final_norm):
    f = lambda a: np.ascontiguousarray(np.asarray(a, dtype=np.float32))
    x, c, ctx, c_ctx = f(x), f(c), f(ctx), f(c_ctx)
    w_mod, b_mod, norm1, w_in, na_rpb = f(w_mod), f(b_mod), f(norm1), f(w_in), f(na_rpb)
    q_gain, k_gain, w_pa, w_pb, w_o = f(q_gain), f(k_gain), f(w_pa), f(w_pb), f(w_o)
    norm2, w_ffn_in, w_ffn_out, final_norm = f(norm2), f(w_ffn_in), f(w_ffn_out), f(final_norm)

    if "nc" not in _CACHE:
        _CACHE["nc"] = build_program(n_layers=_N_LAYERS, debug=_DEBUG)
        _CACHE["dbg_names"] = list(getattr(_CACHE["nc"], "_dbg_names", []))
    nc = _CACHE["nc"]
    NL = _N_LAYERS

    fm = lambda v: np.ascontiguousarray(v.reshape(KC, 128).T)
    norm1T = np.ascontiguousarray(norm1.reshape(DEPTH, KC, 128).transpose(2, 0, 1))
    norm2T = np.ascontiguousarray(norm2.reshape(DEPTH, KC, 128).transpose(2, 0, 1))
    bmodT = np.ascontiguousarray(b_mod.reshape(DEPTH, 48, 128).transpose(2, 0, 1))
    fnT = fm(final_norm)
    pidx = np.arange(128) % 64
    qgT = np.ascontiguousarray(q_gain[:, pidx].T)
    kgT = np.ascontiguousarray(k_gain[:, pidx].T)
    cos_all, sin_all = _rope_tables(8192)
    perm, onehot = _consts()
    tb = _bias_tables(na_rpb)
    shared = dict(ident=np.eye(128, dtype=np.float32), norm1T=norm1T, norm2T=norm2T, bmodT=bmodT, fnT=fnT, qgainT=qgT, kgainT=kgT, perm=perm,
                  onehot=onehot, tb=tb, w_mod=w_mod[:NL], w_in=w_in[:NL], w_pa=w_pa[:NL], w_pb=w_pb[:NL],
                  w_o=w_o[:NL], w_ffn_in=w_ffn_in[:NL], w_ffn_out=w_ffn_out[:NL])
    in_maps = []
    for i in range(NCORES):
        b, j = i // 4, i % 4
        xs = x[b, j * T_OWN:(j + 1) * T_OWN]
        xT = np.ascontiguousarray(np.concatenate([xs.T, ctx[b].T], axis=1))
        cvec = np.ascontiguousarray(np.stack([fm(c[b]), fm(c_ctx)], axis=-1))
        tsl = slice(j * T_OWN, (j + 1) * T_OWN)
        pair = (np.arange(128) % 64) // 2
        cosT = np.ascontiguousarray(cos_all[tsl][:, pair].T)
        sinT = np.ascontiguousarray(sin_all[tsl][:, pair].T)
        sel = np.zeros((128, 8), np.float32)
        if j > 0:
            sel[:, j - 1] = 1.0
        if j < 3:
            sel[:, 4 + j + 1] = 1.0
        m = dict(shared)
        m.update(xT_in=xT, cvec=cvec, cosT=cosT, sinT=sinT, sel=sel, pen=_pen_table(j))
        in_maps.append(m)
    res = run_bass_kernel_spmd(nc, in_maps, core_ids=list(range(NCORES)))
    if _DEBUG is not None:
        return {n: np.stack([res.results[i][n] for i in range(NCORES)]) for n in _CACHE["dbg_names"]}
    out = np.empty((2, 8192, D), np.float32)
    for i in range(NCORES):
        b, j = i // 4, i % 4
        out[b, j * T_OWN:(j + 1) * T_OWN] = res.results[i]["yT_out"].T
    return out
```

```python
import numpy as np
import concourse.bass as bass
import concourse.mybir as mybir
from concourse.bass_utils import run_bass_kernel_spmd

F32 = mybir.dt.float32
BF16 = mybir.dt.bfloat16
AF = mybir.ActivationFunctionType
ALU = mybir.AluOpType

NCORES = 8
D = 1024
KC = 8
DEPTH = 4
T_OWN = 2048
T_CTX = 256
TT = T_OWN + T_CTX
GRID_W = 64
HD = 64
DFF = 2816
FC = DFF // 128
IN_COLS = 4352
NEG = -30000.0
EPS = 1e-6
CH = [(0, 512), (512, 512), (1024, 512), (1536, 512), (2048, 256)]
BLK = [[0, 1], [2, 3, 4]]
XROW = 2080


class Buf:
    __slots__ = ("name", "w", "r")

    def __init__(self, name):
        self.name = name
        self.w = {}
        self.r = {}


class DSem:
    __slots__ = ("sem", "cnt", "key")

    def __init__(self, nc, name):
        self.sem = nc.alloc_semaphore(name)
        self.cnt = 0
        self.key = name


class Eng:
    def __init__(self, nc, name, h):
        self.name = name
        self.h = h
        self.key = "E_" + name
        self.sem = nc.alloc_semaphore("sem_" + name)
        self.cnt = 0
        self.waited = {}


class KB:
    def __init__(self, nc):
        self.nc = nc
        self.E = {
            "pe": Eng(nc, "pe", nc.tensor),
            "act": Eng(nc, "act", nc.scalar),
            "dve": Eng(nc, "dve", nc.vector),
            "pool": Eng(nc, "pool", nc.gpsimd),
            "sp": Eng(nc, "sp", nc.sync),
        }
        self.dsems = []
        self.n_wait = 0
        self.n_ins = 0

    def dsem(self, name):
        d = DSem(self.nc, "D_" + name)
        self.dsems.append(d)
        return d

    @staticmethod
    def _merge(need, d):
        for key, sv in d.items():
            if key not in need or need[key][1] < sv[1]:
                need[key] = sv

    def _wait(self, e, reads, writes, skip_key=None):
        need = {}
        for b in reads:
            self._merge(need, b.w)
        oth = {}
        for b in writes:
            self._merge(oth, b.w)
            self._merge(oth, b.r)
        for key, sv in oth.items():
            if key not in need or need[key][1] < sv[1]:
                need[key] = sv
        for key, (sem, val) in need.items():
            if key == e.key and e.name == "pe":
                continue
            if key == skip_key:
                continue
            if e.waited.get(key, 0) >= val:
                continue
            e.h.wait_ge(sem, val)
            e.waited[key] = val
            self.n_wait += 1

    def op(self, eng, fn, reads=(), writes=()):
        e = self.E[eng]
        self._wait(e, reads, writes)
        ins = fn(e.h)
        e.cnt += 1
        ins.then_inc(e.sem, 1)
        self.n_ins += 1
        rec = (e.sem, e.cnt)
        for b in reads:
            b.r[e.key] = rec
        for b in writes:
            b.w[e.key] = rec
        return ins

    def dma(self, q, ds, out, in_, reads=(), writes=()):
        e = self.E[q]
        self._wait(e, reads, writes, skip_key=ds.key)
        ins = e.h.dma_start(out=out, in_=in_)
        ds.cnt += 16
        ins.then_inc(ds.sem, 16)
        self.n_ins += 1
        rec = (ds.sem, ds.cnt)
        for b in reads:
            b.r[ds.key] = rec
        for b in writes:
            b.w[ds.key] = rec
        return ins

    def seal(self, ds, bufs):
        rec = (ds.sem, ds.cnt)
        for b in bufs:
            if ds.key in b.w:
                b.w[ds.key] = rec
            if ds.key in b.r:
                b.r[ds.key] = rec

    def barrier(self):
        for e in self.E.values():
            for f in self.E.values():
                if f is e or f.cnt == 0 or f.name == "sp":
                    continue
                if e.waited.get(f.key, 0) < f.cnt:
                    e.h.wait_ge(f.sem, f.cnt)
                    e.waited[f.key] = f.cnt
                    self.n_wait += 1
            for d in self.dsems:
                if d.cnt and e.waited.get(d.key, 0) < d.cnt:
                    e.h.wait_ge(d.sem, d.cnt)
                    e.waited[d.key] = d.cnt
                    self.n_wait += 1


def build_program(n_layers=DEPTH, debug=None):
    nc = bass.Bass("TRN2", target_bir_lowering=False)
    k = KB(nc)

    def din(name, shape, dt=F32):
        return nc.dram_tensor(name, list(shape), dt, kind="ExternalInput").ap()

    x_in = din("xT_in", [D, TT])
    cvec_in = din("cvec", [128, KC, 2])
    norm1_in = din("norm1T", [128, DEPTH, KC])
    norm2_in = din("norm2T", [128, DEPTH, KC])
    bmod_in = din("bmodT", [128, DEPTH, 48])
    fn_in = din("fnT", [128, KC])
    qg_in = din("qgainT", [128, DEPTH])
    kg_in = din("kgainT", [128, DEPTH])
    cos_in = din("cosT", [128, T_OWN])
    sin_in = din("sinT", [128, T_OWN])
    perm_in = din("perm", [128, 128])
    ident_in = din("ident", [128, 128])
    onehot_in = din("onehot", [128, 8 * 128])
    pen_in = din("pen", [128, 4 * 512])
    sel_in = din("sel", [128, 8])
    tb_in = din("tb", [DEPTH, 4, 128, 2 * 22 * 64])
    w_mod = din("w_mod", [n_layers, D, 6 * D])
    w_in = din("w_in", [n_layers, D, IN_COLS])
    w_pa = din("w_pa", [n_layers, 512, D])
    w_pb = din("w_pb", [n_layers, 512, D])
    w_o = din("w_o", [n_layers, D, D])
    w_f1 = din("w_ffn_in", [n_layers, D, 2 * DFF])
    w_f2 = din("w_ffn_out", [n_layers, DFF, D])
    y_out = nc.dram_tensor("yT_out", [D, T_OWN], F32, kind="ExternalOutput").ap()
    dbg_names = []

    x_home = nc.dram_tensor("x_home", [128, KC * TT], F32)
    XW = [2048, XROW, 2048, XROW]
    xin = [nc.dram_tensor("xchg_in%d" % p, [128, XW[p]], BF16) for p in range(4)]
    xout = [nc.dram_tensor("xchg_out%d" % p, [4 * 128, XW[p]], BF16) for p in range(4)]

    base = (nc.sbuf_base + 63) // 64 * 64
    total = (nc.sbuf_top - base) // 64 * 64
    nc.alloc_sbuf_tensor("arena", [128, total], mybir.dt.uint8)

    def at(name, shape, dt, off):
        return nc.alloc_sbuf_tensor_at(name, list(shape), dt, offset=base + off)

    A = 0
    xT = at("xT", [128, KC, TT], F32, A)
    kT_na = at("kT_na", [128, 4, 2816], BF16, A)
    v_na = at("v_na", [128, 22, 520], BF16, A + 22528)
    kT_g = at("kT_g", [128, 8448], BF16, A)
    v_g = at("v_g", [128, 66, 130], BF16, A + 16896)
    cosT = at("cosT", [128, T_OWN], F32, A + 45440)
    sinT = at("sinT", [128, T_OWN], F32, A + 53632)
    kst = at("kst", [128, TT], BF16, A + 61824)
    vst = at("vst", [128, 18, 130], BF16, A + 66432)
    C0 = 73728
    co = [C0]

    def cat(name, shape, dt, nbytes):
        t = at(name, shape, dt, co[0])
        co[0] += (nbytes + 63) // 64 * 64
        return t

    ones_bf = cat("ones_bf", [128, 128], BF16, 256)
    bd_bf = cat("bd_bf", [128, 128], BF16, 256)
    ident_bf = cat("ident_bf", [128, 128], BF16, 256)
    sel64_bf = cat("sel64_bf", [128, 128], BF16, 256)
    perm_f = cat("perm_f", [128, 128], F32, 512)
    onehot = cat("onehot", [128, 8 * 128], BF16, 2048)
    pen = cat("pen", [128, 4 * 512], BF16, 4096)
    sel = cat("sel", [128, 8], F32, 32)
    eps_t = cat("eps_t", [128, 1], F32, 4)
    norm1T = cat("norm1T", [128, DEPTH, KC], F32, 128)
    norm2T = cat("norm2T", [128, DEPTH, KC], F32, 128)
    bmodT = cat("bmodT", [128, DEPTH, 48], F32, 768)
    fnT = cat("fnT", [128, KC], F32, 32)
    qgT = cat("qgT", [128, DEPTH], F32, 16)
    kgT = cat("kgT", [128, DEPTH], F32, 16)
    cvec = cat("cvec", [128, KC, 2], F32, 64)
    silu_c = cat("silu_c", [128, KC, 2], F32, 64)
    modT_l = [cat("modT0", [128, 48, 2], F32, 384), cat("modT1", [128, 48, 2], F32, 384)]
    A1_l = [cat("A1_0", [128, KC, 2], F32, 64), cat("A1_1", [128, KC, 2], F32, 64)]
    A2_l = [cat("A2_0", [128, KC, 2], F32, 64), cat("A2_1", [128, KC, 2], F32, 64)]
    cur = {}
    assert co[0] <= C0 + 10240, co[0]
    HT = C0 + 10240
    hT = at("hT", [128, KC, TT], BF16, HT)
    Y = HT + 36864
    qn = at("qn", [128, 4, TT], BF16, Y)
    qg = at("qg", [128, 4, TT], BF16, Y + 18432)
    FR = Y + 36864
    W = total - 33792
    FSZ = W - FR
    assert FSZ >= 21248, FSZ

    def fat(name, shape, dt, off):
        return at(name, shape, dt, FR + off)

    stage = [at("stage0", [128, 2816], F32, W), at("stage1", [128, 2816], F32, W + 11264)]
    slot = [at("slot0", [128, 2816], BF16, W + 22528), at("slot1", [128, 2816], BF16, W + 28160)]
    tbias = [at("tbias0", [128, 2, 22, 64], F32, W), at("tbias1", [128, 2, 22, 64], F32, W + 11264)]
    tb16 = [at("tb16_0", [128, 2, 22, 64], BF16, W + 22528), at("tb16_1", [128, 2, 22, 64], BF16, W + 28160)]
    cstage_f = at("cstage_f", [128, 4 * 512], F32, W)

    psall = nc.alloc_psum_tensor("psall", [128, 8 * 512], F32)
    ps = [psall[:, i * 512:(i + 1) * 512] for i in range(8)]
    PSB = [Buf("ps%d" % i) for i in range(8)]

    B = {}

    def bf(name):
        if name not in B:
            B[name] = Buf(name)
        return B[name]

    d_const = k.dsem("const")
    d_x = k.dsem("x")
    d_stage = [k.dsem("st0"), k.dsem("st1")]
    d_tb = [k.dsem("tb0"), k.dsem("tb1")]
    d_xh = k.dsem("xhome")
    d_xin = k.dsem("xin")
    d_kv = k.dsem("kvload")
    d_halo = [k.dsem("halo0"), k.dsem("halo1")]
    d_rope = k.dsem("rope")
    d_out = k.dsem("out")
    ccsem = nc.alloc_semaphore("ccsem")
    cc_cnt = [0]

    wl_state = {"i": 0}

    def wload(srcs, shape, tag):
        i = wl_state["i"] % 2
        wl_state["i"] += 1
        n = shape[1] * shape[2]
        st_v = stage[i][:, 0:n].rearrange("p (a b) -> p a b", a=shape[1])
        sl_v = slot[i][:, 0:n].rearrange("p (a b) -> p a b", a=shape[1])
        sb, lb = bf("stage%d" % i), bf("slot%d" % i)
        for fn, src in srcs:
            k.dma("sp", d_stage[i], fn(st_v), src, writes=[sb])
        k.seal(d_stage[i], [sb])
        k.op("dve", lambda e: e.tensor_copy(out=slot[i][:, 0:n], in_=stage[i][:, 0:n]), reads=[sb], writes=[lb])
        return sl_v, lb

    def w_kc(wap, c0, ncols):
        return wap[:, c0:c0 + ncols].rearrange("(kc p) n -> p kc n", p=128)

    def full(v):
        return v

    psr = {"i": 0}

    def next_ps(lo=0, hi=8):
        i = lo + psr["i"] % (hi - lo)
        psr["i"] += 1
        return i

    def colsel(c):
        return 1 if c == 4 else 0

    cb = bf("consts")
    k.dma("sp", d_const, cvec[:], cvec_in, writes=[cb])
    k.dma("sp", d_const, norm1T[:], norm1_in, writes=[cb])
    k.dma("sp", d_const, norm2T[:], norm2_in, writes=[cb])
    k.dma("sp", d_const, bmodT[:], bmod_in, writes=[cb])
    k.dma("sp", d_const, fnT[:], fn_in, writes=[cb])
    k.dma("sp", d_const, qgT[:], qg_in, writes=[cb])
    k.dma("sp", d_const, kgT[:], kg_in, writes=[cb])
    k.dma("sp", d_const, perm_f[:], perm_in, writes=[cb])
    k.dma("sp", d_const, sel[:], sel_in, writes=[cb])
    k.dma("sp", d_const, cstage_f[:, :], pen_in, writes=[cb, bf("stage0")])
    k.seal(d_const, [cb, bf("stage0")])
    k.op("dve", lambda e: e.tensor_copy(out=pen[:, :], in_=cstage_f[:, :]), reads=[cb, bf("stage0")], writes=[bf("pen")])
    k.dma("sp", d_const, cstage_f[:, 0:1024], onehot_in, reads=[bf("pen")], writes=[bf("stage0")])
    k.op("dve", lambda e: e.tensor_copy(out=onehot[:, :], in_=cstage_f[:, 0:1024]), reads=[bf("stage0")], writes=[bf("onehot")])
    k.dma("sp", d_const, cstage_f[:, 0:128], ident_in, reads=[bf("onehot")], writes=[bf("stage0")])
    k.op("dve", lambda e: e.tensor_copy(out=ident_bf[:, :], in_=cstage_f[:, 0:128]), reads=[bf("stage0")], writes=[bf("onehot")])
    k.op("pool", lambda e: e.memset(ones_bf[:], 1.0), writes=[cb])
    k.op("pool", lambda e: e.memset(eps_t[:], EPS), writes=[cb])
    k.op("pool", lambda e: e.memset(sel64_bf[:], 0.0), writes=[cb])
    k.op("pool", lambda e: e.memset(sel64_bf[64:65, :], 1.0), writes=[cb])
    k.op("pool", lambda e: e.memset(bd_bf[:], 0.0), writes=[cb])
    k.op("pool", lambda e: e.memset(bd_bf[0:64, 0:64], 1.0), writes=[cb])
    k.op("pool", lambda e: e.memset(bd_bf[64:128, 64:128], 1.0), writes=[cb])
    k.op("act", lambda e: e.activation(out=silu_c[:], in_=cvec[:], func=AF.Silu), reads=[cb], writes=[cb])
    k.op("dve", lambda e: e.tensor_scalar(out=qgT[:], in0=qgT[:], scalar1=0.125, scalar2=None, op0=ALU.mult),
         reads=[cb], writes=[cb])
    XB = [bf("x%d" % c) for c in range(5)]
    for c, (t0, tw) in enumerate(CH):
        k.dma("sp", d_x, xT[:, :, t0:t0 + tw], x_in[:, t0:t0 + tw].rearrange("(kc p) t -> p kc t", p=128), writes=[XB[c]])
    k.seal(d_x, XB)
    k.barrier()

    def set_layer(l):
        cur["modT"], cur["A1"], cur["A2"], cur["mb"] = modT_l[l % 2], A1_l[l % 2], A2_l[l % 2], bf("mod%d" % (l % 2))

    MODB = 7

    mstage = [stage[0], stage[1]]
    mtok = ["stage0", "stage1"]
    mds = [d_stage[0], d_stage[1]]
    mstA = [at("mstA0", [128, 2048], F32, A + 45440), at("mstA1", [128, 2048], F32, A + 45440 + 8192)]
    d_mst = [k.dsem("mst0"), k.dsem("mst1")]

    def mod_use_alt(alt):
        if alt:
            mstage[:] = mstA; mtok[:] = ["mstA0", "mstA1"]; mds[:] = d_mst
        else:
            mstage[:] = [stage[0], stage[1]]; mtok[:] = ["stage0", "stage1"]; mds[:] = [d_stage[0], d_stage[1]]

    def mod_dma(l, g):
        i = g % 2
        st_v = mstage[i][:, 0:2048].rearrange("p (a b) -> p a b", a=8)
        k.dma("sp", mds[i], st_v, w_kc(w_mod[l], g * 256, 256), writes=[bf(mtok[i])])

    def mod_mm(l, g):
        i = g % 2
        sb = bf(mtok[i])
        st_v = mstage[i][:, 0:2048].rearrange("p (a b) -> p a b", a=8)
        for cc in range(2):
            ch = g * 2 + cc
            for kc in range(KC):
                k.op("pe", lambda e, kc=kc, cc=cc, ch=ch: e.matmul(
                    ps[MODB][:, ch * 2:ch * 2 + 2], lhsT=st_v[:, kc, cc * 128:(cc + 1) * 128], rhs=silu_c[:, kc, :],
                    start=(kc == 0), stop=(kc == KC - 1)), reads=[sb, cb], writes=[PSB[MODB]])

    def mod_finish(l):
        mT, a1, a2, mb = modT_l[l % 2], A1_l[l % 2], A2_l[l % 2], bf("mod%d" % (l % 2))
        for col in range(2):
            k.op("dve", lambda e, col=col: e.tensor_tensor(
                out=mT[:, :, col], in0=ps[MODB][:, 0:96].rearrange("p (c t) -> p c t", t=2)[:, :, col],
                in1=bmodT[:, l, :], op=ALU.add), reads=[PSB[MODB], cb], writes=[mb])
        for col in range(2):
            k.op("dve", lambda e, col=col: e.scalar_tensor_tensor(
                out=a1[:, :, col], in0=mT[:, 8:16, col], scalar=1.0, in1=norm1T[:, l, :], op0=ALU.add, op1=ALU.mult),
                reads=[mb, cb], writes=[mb])
            k.op("dve", lambda e, col=col: e.scalar_tensor_tensor(
                out=a2[:, :, col], in0=mT[:, 32:40, col], scalar=1.0, in1=norm2T[:, l, :], op0=ALU.add, op1=ALU.mult),
                reads=[mb, cb], writes=[mb])

    def phase_mod(l):
        mod_dma(l, 0)
        for g in range(24):
            if g + 1 < 24:
                mod_dma(l, g + 1)
            mod_mm(l, g)
        mod_finish(l)

    nst = {"i": 0}

    def norm_stats(c, off0, pfx):
        t0, tw = CH[c]
        i = nst["i"] % 2
        nst["i"] += 1
        sq = at(pfx + "sq", [128, KC, 512], BF16, off0)
        rstd = at(pfx + "rstd%d" % i, [128, 512], F32, off0 + 8192 + (0 if i == 0 else 6144))
        sqb, rb = bf(pfx + "sq"), bf(pfx + "rstd%d" % i)
        k.op("act", lambda e: e.activation(out=sq[:, :, 0:tw], in_=xT[:, :, t0:t0 + tw], func=AF.Square),
             reads=[XB[c]], writes=[sqb])
        pb = 7
        for kc in range(KC):
            k.op("pe", lambda e, kc=kc: e.matmul(ps[pb][:, 0:tw], lhsT=ones_bf[:], rhs=sq[:, kc, 0:tw],
                                                 start=(kc == 0), stop=(kc == KC - 1)),
                 reads=[sqb, cb], writes=[PSB[pb]])
        k.op("act", lambda e: e.activation(out=rstd[:, 0:tw], in_=ps[pb][:, 0:tw], func=AF.Sqrt, scale=1.0 / D,
                                           bias=eps_t[:, 0:1]), reads=[PSB[pb], cb], writes=[rb])
        k.op("dve", lambda e: e.reciprocal(out=rstd[:, 0:tw], in_=rstd[:, 0:tw]), reads=[rb], writes=[rb])
        return rstd, rb

    def norm_apply(c, st, Aw, shift_idx, out_t, out_off, tagbuf, off0, pfx):
        t0, tw = CH[c]
        col = colsel(c)
        mb = cur["mb"]
        modT = cur["modT"]
        rstd, rb = st
        tmp = [at(pfx + "tmp0", [128, 512], F32, off0 + 10240), at(pfx + "tmp1", [128, 512], F32, off0 + 12288)]
        tb_ = [bf(pfx + "tmp0"), bf(pfx + "tmp1")]
        for kc in range(KC):
            i = kc % 2
            k.op("dve", lambda e, kc=kc, i=i: e.scalar_tensor_tensor(
                out=tmp[i][:, 0:tw], in0=xT[:, kc, t0:t0 + tw], scalar=Aw[:, kc, col:col + 1], in1=rstd[:, 0:tw],
                op0=ALU.mult, op1=ALU.mult), reads=[XB[c], mb, rb], writes=[tb_[i]])
            k.op("act", lambda e, kc=kc, i=i: e.activation(
                out=out_t[:, kc, out_off:out_off + tw], in_=tmp[i][:, 0:tw], func=AF.Identity,
                bias=modT[:, shift_idx * 8 + kc, col:col + 1], scale=1.0), reads=[tb_[i], mb], writes=[tagbuf])

    HB = [bf("h%d" % c) for c in range(5)]

    def phase_norm1(l):
        st = norm_stats(0, FR, "n1")
        for c in range(5):
            nxt = norm_stats(c + 1, FR, "n1") if c + 1 < 5 else None
            norm_apply(c, st, cur["A1"], 0, hT, CH[c][0], HB[c], FR, "n1")
            spill_chunk(c)
            st = nxt

    def spill_chunk(c):
        t0, tw = CH[c]
        k.dma("sp", d_xh, x_home.ap().rearrange("p (kc t) -> p kc t", kc=KC)[:, :, t0:t0 + tw], xT[:, :, t0:t0 + tw],
              reads=[XB[c]], writes=[bf("xhome")])

    def phase_spill():
        pass

    def phase_restore():
        for c, (t0, tw) in enumerate(CH):
            k.dma("sp", d_x, xT[:, :, t0:t0 + tw], x_home.ap().rearrange("p (kc t) -> p kc t", kc=KC)[:, :, t0:t0 + tw],
                  reads=[bf("xhome")], writes=[XB[c]])
        k.seal(d_x, XB)

    def proj_fm(sl_v, lb, cc, c, pb):
        t0, tw = CH[c]
        for kc in range(KC):
            k.op("pe", lambda e, kc=kc: e.matmul(ps[pb][:, 0:tw], lhsT=sl_v[:, kc, cc * 128:(cc + 1) * 128],
                                                 rhs=hT[:, kc, t0:t0 + tw], start=(kc == 0), stop=(kc == KC - 1)),
                 reads=[lb, HB[c]], writes=[PSB[pb]])

    nr = {"i": 0, "q": []}

    def gqa_normrope(l, pb, c, gain_t, out_ap, out_buf, is_q):
        t0, tw = CH[c]
        i = nr["i"] % 2
        nr["i"] += 1
        o = i * 9216
        rg = fat("rg%d" % i, [128, 512], F32, o)
        sqh = fat("sqh%d" % i, [128, 512], BF16, o + 2048)
        rs_ = fat("rs_%d" % i, [128, 512], F32, o + 3072)
        t1 = fat("t1%d" % i, [128, 512], F32, o + 5120)
        t2 = fat("t2%d" % i, [128, 512], F32, o + 7168)
        rgb, sqb, rsb, t1b, t2b = bf("rg%d" % i), bf("sqh%d" % i), bf("rs_%d" % i), bf("t1%d" % i), bf("t2%d" % i)
        rp = bf("rope")

        def s1():
            k.op("act", lambda e: e.activation(out=rg[:, 0:tw], in_=ps[pb][:, 0:tw], func=AF.Copy, scale=gain_t[:, l:l + 1]),
                 reads=[PSB[pb], cb], writes=[rgb])
            k.op("act", lambda e: e.activation(out=sqh[:, 0:tw], in_=ps[pb][:, 0:tw], func=AF.Square),
                 reads=[PSB[pb]], writes=[sqb])

        def s2():
            k.op("pe", lambda e: e.matmul(ps[6][:, 0:tw], lhsT=bd_bf[:], rhs=sqh[:, 0:tw], start=True, stop=True),
                 reads=[sqb, cb], writes=[PSB[6]])
            k.op("act", lambda e: e.activation(out=rs_[:, 0:tw], in_=ps[6][:, 0:tw], func=AF.Sqrt, scale=1.0 / HD,
                                               bias=eps_t[:, 0:1]), reads=[PSB[6], cb], writes=[rsb])
            k.op("dve", lambda e: e.reciprocal(out=rs_[:, 0:tw], in_=rs_[:, 0:tw]), reads=[rsb], writes=[rsb])
            if c != 4:
                k.op("pe", lambda e: e.matmul(ps[7][:, 0:tw], lhsT=perm_f[:], rhs=rg[:, 0:tw], start=True, stop=True),
                     reads=[rgb, cb], writes=[PSB[7]])
                k.op("pool", lambda e: e.tensor_tensor(out=t1[:, 0:tw], in0=rg[:, 0:tw], in1=cosT[:, t0:t0 + tw], op=ALU.mult),
                     reads=[rgb, rp], writes=[t1b])

        def s3():
            if c == 4:
                k.op("dve", lambda e: e.tensor_tensor(out=out_ap, in0=rg[:, 0:tw], in1=rs_[:, 0:tw], op=ALU.mult),
                     reads=[rgb, rsb], writes=[out_buf])
                return
            k.op("dve", lambda e: e.tensor_tensor(out=t2[:, 0:tw], in0=ps[7][:, 0:tw], in1=sinT[:, t0:t0 + tw], op=ALU.mult),
                 reads=[PSB[7], rp], writes=[t2b])
            k.op("dve", lambda e: e.tensor_tensor(out=t1[:, 0:tw], in0=t1[:, 0:tw], in1=t2[:, 0:tw], op=ALU.add),
                 reads=[t1b, t2b], writes=[t1b])
            k.op("dve", lambda e: e.tensor_tensor(out=out_ap, in0=t1[:, 0:tw], in1=rs_[:, 0:tw], op=ALU.mult),
                 reads=[t1b, rsb], writes=[out_buf])

        s1()
        q = nr["q"]
        q.append((s2, s3))
        if len(q) > 1:
            p2, p3 = q.pop(0)
            p2()
            p3()

    def normrope_flush():
        q = nr["q"]
        while q:
            p2, p3 = q.pop(0)
            p2()
            p3()

    KNB = bf("kT_na")
    VNB = bf("v_na")
    QNB = [[bf("qn_%d_%d" % (pr, c)) for c in range(5)] for pr in range(4)]
    QGB = [[bf("qg_%d_%d" % (rr, c)) for c in range(5)] for rr in range(4)]
    KSB, VSB = bf("kst"), bf("vst")

    def phase_proj_kv(l):
        rp = bf("rope")
        k.dma("sp", d_rope, cosT[:], cos_in, writes=[rp])
        k.dma("sp", d_rope, sinT[:], sin_in, writes=[rp])
        k.seal(d_rope, [rp])
        k.op("pool", lambda e: e.memset(v_na[:, :, :].rearrange("p t (h c) -> p t h c", c=65)[:, :, :, 64:65], 1.0),
             writes=[VNB])
        k.op("pool", lambda e: e.memset(vst[:, :, :].rearrange("p t (h c) -> p t h c", c=65)[:, :, :, 64:65], 1.0),
             writes=[VSB])
        wl = w_in[l]
        sl_v, lb = wload([(full, w_kc(wl, 2048, 256))], [128, KC, 256], "gkv")
        for c in range(5):
            t0, tw = CH[c]
            pb = next_ps(0, 4)
            proj_fm(sl_v, lb, 0, c, pb)
            gqa_normrope(l, pb, c, kgT, kst[:, t0:t0 + tw], KSB, False)
        normrope_flush()
        for tile in range(18):
            c = tile // 4
            pb = next_ps(0, 4)
            for kc in range(KC):
                k.op("pe", lambda e, kc=kc, tile=tile, pb=pb: e.matmul(
                    ps[pb][:, 0:128], lhsT=hT[:, kc, tile * 128:(tile + 1) * 128], rhs=sl_v[:, kc, 128:256],
                    start=(kc == 0), stop=(kc == KC - 1)), reads=[lb, HB[c]], writes=[PSB[pb]])
            dst = vst[:, tile, :].rearrange("p (h c) -> p h c", c=65)[:, :, 0:64]
            src = ps[pb][:, 0:128].rearrange("p (h c) -> p h c", c=64)
            k.op("act", lambda e, dst=dst, src=src: e.activation(out=dst, in_=src, func=AF.Copy), reads=[PSB[pb]], writes=[VSB])
        for grp in range(2):
            sl_v, lb = wload([(full, w_kc(wl, 512 + grp * 256, 256))], [128, KC, 256], "nak")
            for cc in range(2):
                pr = grp * 2 + cc
                for c in range(5):
                    t0, tw = CH[c]
                    pb = next_ps(0, 4)
                    proj_fm(sl_v, lb, cc, c, pb)
                    ko = 256 + t0 if c < 4 else 2560
                    k.op("act", lambda e, pr=pr, pb=pb, ko=ko, tw=tw: e.activation(
                        out=kT_na[:, pr, ko:ko + tw], in_=ps[pb][:, 0:tw], func=AF.Copy), reads=[PSB[pb]], writes=[KNB])
        for grp in range(2):
            sl_v, lb = wload([(full, w_kc(wl, 1024 + grp * 256, 256))], [128, KC, 256], "nav")
            for tile in range(18):
                c = tile // 4
                pb = next_ps(0, 4)
                for kc in range(KC):
                    k.op("pe", lambda e, kc=kc, tile=tile, pb=pb: e.matmul(
                        ps[pb][:, 0:256], lhsT=hT[:, kc, tile * 128:(tile + 1) * 128], rhs=sl_v[:, kc, :],
                        start=(kc == 0), stop=(kc == KC - 1)), reads=[lb, HB[c]], writes=[PSB[pb]])
                ti = tile + 2 if tile < 16 else tile + 4
                dst = v_na[:, ti, :].rearrange("p (h c) -> p h c", c=65)[:, grp * 4:(grp + 1) * 4, 0:64]
                src = ps[pb][:, 0:256].rearrange("p (h c) -> p h c", c=64)
                k.op("act", lambda e, dst=dst, src=src: e.activation(out=dst, in_=src, func=AF.Copy),
                     reads=[PSB[pb]], writes=[VNB])

    def phase_proj_q(l, qch):
        wl = w_in[l]
        for grp in range(2):
            sl_v, lb = wload([(full, w_kc(wl, grp * 256, 256))], [128, KC, 256], "naq")
            for cc in range(2):
                pr = grp * 2 + cc
                for c in qch:
                    t0, tw = CH[c]
                    pb = next_ps(0, 4)
                    proj_fm(sl_v, lb, cc, c, pb)
                    k.op("act", lambda e, pr=pr, pb=pb, t0=t0, tw=tw: e.activation(
                        out=qn[:, pr, t0:t0 + tw], in_=ps[pb][:, 0:tw], func=AF.Copy, scale=0.125),
                        reads=[PSB[pb]], writes=[QNB[pr][c]])
        for grp in range(2):
            srcs = []
            for g in range(2):
                for r in range(2):
                    c0 = 1536 + g * 256 + (grp * 2 + r) * 64
                    src = wl[:, c0:c0 + 64].rearrange("(kc p) d -> p kc d", p=128)
                    srcs.append((lambda v, g=g, r=r: v.rearrange("p kc (r g d) -> p kc r g d", g=2, d=64)[:, :, r, g, :], src))
            sl_v, lb = wload(srcs, [128, KC, 256], "gq")
            for cc in range(2):
                rr = grp * 2 + cc
                for c in qch:
                    t0, tw = CH[c]
                    pb = next_ps(0, 4)
                    proj_fm(sl_v, lb, cc, c, pb)
                    gqa_normrope(l, pb, c, qgT, qg[:, rr, t0:t0 + tw], QGB[rr][c], True)
        normrope_flush()

    def phase_exchange():
        xb = [bf("xin%d" % p) for p in range(4)]
        xi = [t.ap() for t in xin]
        k.dma("sp", d_xin, xi[0][:, 0:2048], kst[:, 0:2048], reads=[KSB], writes=[xb[0]])
        k.dma("sp", d_xin, xi[1][:, :], vst[:, 0:16, :].rearrange("p t c -> p (t c)"), reads=[VSB], writes=[xb[1]])
        kb_v = xi[2][:, 0:2048].rearrange("p (tb pr c) -> p tb pr c", tb=2, pr=4)
        k.dma("sp", d_xin, kb_v[:, 0, :, :], kT_na[:, :, 256:512], reads=[KNB], writes=[xb[2]])
        k.dma("sp", d_xin, kb_v[:, 1, :, :], kT_na[:, :, 2048:2304], reads=[KNB], writes=[xb[2]])
        k.dma("sp", d_xin, xi[3][:, 0:1040], v_na[:, 2:4, :].rearrange("p t c -> p (t c)"), reads=[VNB], writes=[xb[3]])
        k.dma("sp", d_xin, xi[3][:, 1040:2080], v_na[:, 16:18, :].rearrange("p t c -> p (t c)"), reads=[VNB], writes=[xb[3]])
        k.seal(d_xin, xb)
        e = k.E["pool"]
        for p in (2, 3, 0, 1):
            ob = bf("xout%d" % p)
            k._wait(e, [xb[p]], [ob])
            ins = nc.gpsimd.collective_compute("AllGather", ALU.bypass, replica_groups=[[0, 1, 2, 3], [4, 5, 6, 7]],
                                               ins=[xin[p].ap().opt()], outs=[xout[p].ap().opt()])
            cc_cnt[0] += 1
            ins.then_inc(ccsem, 1)
            k.n_ins += 1
            ob.w["cc"] = (ccsem, cc_cnt[0])
            xb[p].r["cc"] = (ccsem, cc_cnt[0])

    def phase_halo():
        for r in range(4):
            i = r % 2
            sk = fat("stgk%d" % i, [128, XROW], BF16, i * 8320)
            sv = fat("stgv%d" % i, [128, XROW], BF16, i * 8320 + 4160)
            sb = bf("stg%d" % i)
            k.dma("sp", d_halo[i], sk[:, 0:2048], xout[2].ap()[r * 128:(r + 1) * 128, :], reads=[bf("xout2")], writes=[sb])
            k.dma("sp", d_halo[i], sv[:], xout[3].ap()[r * 128:(r + 1) * 128, :], reads=[bf("xout3")], writes=[sb])
            k.seal(d_halo[i], [sb])
            skv = sk[:, 0:2048].rearrange("p (tb pr c) -> p tb pr c", tb=2, pr=4)
            pieces = [
                (kT_na[:, :, 0:256], skv[:, 1, :, :], r, KNB),
                (kT_na[:, :, 2304:2560], skv[:, 0, :, :], 4 + r, KNB),
                (v_na[:, 0:2, :].rearrange("p t c -> p (t c)"), sv[:, 1040:2080], r, VNB),
                (v_na[:, 18:20, :].rearrange("p t c -> p (t c)"), sv[:, 0:1040], 4 + r, VNB),
            ]
            for dst, src, si, db in pieces:
                if r == 0:
                    k.op("dve", lambda e, dst=dst, src=src, si=si: e.tensor_scalar(
                        out=dst, in0=src, scalar1=sel[:, si:si + 1], scalar2=None, op0=ALU.mult),
                        reads=[sb, cb], writes=[db])
                else:
                    k.op("dve", lambda e, dst=dst, src=src, si=si: e.scalar_tensor_tensor(
                        out=dst, in0=src, scalar=sel[:, si:si + 1], in1=dst, op0=ALU.mult, op1=ALU.add),
                        reads=[sb, cb, db], writes=[db])

    att = {"p": 0, "o": 0, "s": 0, "t": 0, "q": 0, "pending": None, "hook": None, "nsg": 2, "look": 1, "obanks": (4, 5), "bcbank": None, "fdelay": 8}

    def att_flush():
        if att["pending"] is not None:
            f = att["pending"]
            att["pending"] = None
            f()
    QP = [fat("qpad%d" % i, [128, 512], BF16, 12288 + i * 1024) for i in range(4)]
    QPB = [bf("qpad%d" % i) for i in range(4)]

    RSP = [fat("rsp%d" % i, [128, 2, 512], BF16, 16384 + i * 2048) for i in range(2)]

    def att_init():
        for i in range(4):
            k.op("pool", lambda e, i=i: e.memset(QP[i][:], 0.0), writes=[QPB[i]])
        for i in range(2):
            k.op("pool", lambda e, i=i: e.memset(RSP[i][:], 0.0), writes=[bf("rsp%d" % i)])

    def attention_phase(calls):
        LOOK = att["look"]
        pt = [fat("pT%d" % i, [128, 2, 512], BF16, i * 2048) for i in range(4)]
        otsb = [fat("otsb%d" % i, [128, 512], F32, 8192 + i * 2048) for i in range(2)]
        rsp = RSP
        rsb = [bf("rsp0"), bf("rsp1")]
        items = []
        for ci, c in enumerate(calls):
            assert len(c["qk"]) % 2 == 0
            for j in range(len(c["qk"]) // 2):
                items.append((ci, j))
        info = {}
        pend = []

        def setup_q(ci):
            c = calls[ci]
            hh = c["part0"] // 64
            qi = hh * 2 + att["q"] % 2
            att["q"] += 1
            qp = QP[qi]
            p0, qw = c["part0"], c["qw"]
            k.op("pool", lambda e: e.tensor_copy(out=qp[p0:p0 + 64, 0:qw], in_=c["q_ap"]), reads=[c["buf"]], writes=[QPB[qi]])
            info[ci] = dict(qi=qi)

        def setup_o(ci):
            oi = att["o"] % 2
            att["o"] += 1
            info[ci].update(oi=oi, ob=att["obanks"][oi], sg={})

        def flush(now, upto_ci):
            keep = []
            for due, pci, f in pend:
                if due <= now or pci <= upto_ci:
                    f()
                else:
                    keep.append((due, pci, f))
            pend[:] = keep

        if calls:
            setup_q(0)
        for it in range(len(items) + LOOK):
            if it < len(items):
                ci, j = items[it]
                c = calls[ci]
                qw = c["qw"]
                if j == 0:
                    if c.get("pre") is not None:
                        c["pre"]()
                    setup_o(ci)
                    if ci + 1 < len(calls):
                        setup_q(ci + 1)
                inf = info[ci]
                qp = QP[inf["qi"]]
                sg = att["s"] % att["nsg"]
                att["s"] += 1
                inf["sg"][j] = sg
                for half in range(2):
                    d = c["qk"][2 * j + half]
                    bank = 2 * sg + half
                    extra = []
                    if d.get("pen") is not None:
                        extra.append((d["pen"][0], d["pen"][1], [bf("onehot"), bf("pen")]))
                    if d.get("bias") is not None:
                        extra.append((ident_bf[:, :], d["bias"], [bf("onehot"), d["biasbuf"]]))
                    k.op("pe", lambda e, d=d, bank=bank, ne=len(extra), qp=qp, qw=qw: e.matmul(
                        ps[bank][:, 0:qw], lhsT=d["kT"], rhs=qp[:, 0:qw], start=True, stop=(ne == 0)),
                        reads=[d["kbuf"], QPB[inf["qi"]]], writes=[PSB[2 * sg]])
                    for xi_, (l_ap, r_ap, rd) in enumerate(extra):
                        k.op("pe", lambda e, bank=bank, l_ap=l_ap, r_ap=r_ap, qw=qw, lastx=(xi_ == len(extra) - 1): e.matmul(
                            ps[bank][:, 0:qw], lhsT=l_ap, rhs=r_ap, start=False, stop=lastx),
                            reads=rd, writes=[PSB[2 * sg]])
                if j == 0 and att["hook"] is not None:
                    att["hook"]()
            flush(it, -1)
            jt = it - LOOK
            if jt >= 0:
                ci, j = items[jt]
                c = calls[ci]
                qw = c["qw"]
                inf = info[ci]
                sg = inf["sg"][j]
                n = len(c["qk"])
                ob_i, oi = inf["ob"], inf["oi"]
                if j == 0:
                    flush(-1, ci - 2)
                pi = att["p"] % 4
                att["p"] += 1
                pbuf = bf("pT%d" % pi)
                s_ap = psall[:, 2 * sg * 512:(2 * sg + 2) * 512].rearrange("p (a b) -> p a b", a=2)[:, :, 0:qw]
                k.op("act", lambda e, s_ap=s_ap, pi=pi, qw=qw: e.activation(out=pt[pi][:, :, 0:qw], in_=s_ap, func=AF.Exp),
                     reads=[PSB[2 * sg]], writes=[pbuf])
                for half in range(2):
                    d = c["qk"][2 * j + half]
                    jj = 2 * j + half
                    k.op("pe", lambda e, d=d, pi=pi, half=half, jj=jj, ob_i=ob_i, qw=qw, n=n: e.matmul(
                        ps[ob_i][0:65, 0:qw], lhsT=d["v"], rhs=pt[pi][:, half, 0:qw], start=(jj == 0), stop=(jj == n - 1)),
                        reads=[d["vbuf"], pbuf], writes=[PSB[ob_i]])
                if 2 * j + 2 == n:
                    def fin_a(c=c, ob_i=ob_i, oi=oi, qw=qw):
                        osb = bf("otsb%d" % oi)
                        k.op("dve", lambda e: e.tensor_copy(out=otsb[oi][0:65, 0:qw], in_=ps[ob_i][0:65, 0:qw]),
                             reads=[PSB[ob_i]], writes=[osb])
                        k.op("dve", lambda e: e.reciprocal(out=otsb[oi][64:65, 0:qw], in_=otsb[oi][64:65, 0:qw]),
                             reads=[osb], writes=[osb])
                        k.op("pool", lambda e: e.tensor_copy(out=rsp[oi][64:65, 0, 0:qw], in_=otsb[oi][64:65, 0:qw]),
                             reads=[osb], writes=[rsb[oi]])
                        k.op("pool", lambda e: e.tensor_tensor(out=rsp[oi][64:65, 1, 0:qw], in0=otsb[oi][64:65, 0:qw],
                                                               in1=rsp[oi][64:65, 0, 0:qw], op=ALU.subtract),
                             reads=[osb, rsb[oi]], writes=[rsb[oi]])

                    def fin_b(c=c, ob_i=ob_i, oi=oi, qw=qw):
                        osb = bf("otsb%d" % oi)
                        out_ap, obuf = c["out_ap"], c["buf"]
                        bcb = att["bcbank"] if att["bcbank"] is not None else ob_i
                        for t_ in range(2):
                            k.op("pe", lambda e, t_=t_: e.matmul(ps[bcb][:, 0:qw], lhsT=sel64_bf[:, :],
                                                                rhs=rsp[oi][:, t_, 0:qw], start=(t_ == 0), stop=(t_ == 1)),
                                 reads=[rsb[oi], cb], writes=[PSB[bcb]])
                        k.op("dve", lambda e: e.tensor_tensor(out=out_ap, in0=otsb[oi][0:64, 0:qw], in1=ps[bcb][0:64, 0:qw],
                                                              op=ALU.mult), reads=[osb, PSB[bcb], obuf], writes=[obuf])
                    pend.append((it + 1, ci, fin_a))
                    pend.append((it + 1 + att["fdelay"], ci, fin_b))
        flush(10 ** 9, 10 ** 9)

    na_state = {}

    def phase_na(l, qch, mod_next):
        ohv = onehot[:, :].rearrange("m (kt n) -> m kt n", kt=8)
        penv = pen[:, :].rearrange("m (u n) -> m u n", u=4)
        att_init()
        att.update(nsg=2, look=1, obanks=(4, 5), bcbank=6, fdelay=5)
        st = {"g": 0}

        def hook():
            g = st["g"]
            if mod_next is None or g >= 24:
                return
            if g == 0:
                mod_dma(mod_next, 0)
            if g + 1 < 24:
                mod_dma(mod_next, g + 1)
            mod_mm(mod_next, g)
            st["g"] = g + 1
        att["hook"] = hook
        na_state["st"], na_state["hook"] = st, hook
        mod_use_alt(True)
        def load_tab(pr):
            i = pr % 2
            tbs = bf("stage%d" % i)
            tbb = bf("slot%d" % i)
            k.dma("sp", d_tb[i], tbias[i][:].rearrange("p a b c -> p (a b c)"), tb_in[l, pr], writes=[tbs])
            k.op("dve", lambda e, i=i: e.tensor_copy(out=tb16[i][:].rearrange("p a b c -> p (a b c)"),
                                                     in_=tbias[i][:].rearrange("p a b c -> p (a b c)")),
                 reads=[tbs], writes=[tbb])

        calls = []
        for pr in range(4):
            i = pr % 2
            tbb = bf("slot%d" % i)
            first = True
            for hh in range(2):
                p0 = hh * 64
                h = 2 * pr + hh
                for u in qch:
                    t0, qw = CH[u]
                    lst = []
                    if u < 4:
                        for kt in range(8):
                            kc0 = 512 * u + 128 * kt
                            lst.append(dict(
                                kT=kT_na[:, pr, kc0:kc0 + 128], kbuf=KNB,
                                v=v_na[:, 4 * u + kt, h * 65:(h + 1) * 65], vbuf=VNB,
                                bias=tb16[i][:, hh, 14 - 2 * kt:22 - 2 * kt, :].rearrange("p a b -> p (a b)"), biasbuf=tbb,
                                pen=(ohv[:, kt, :], penv[:, u, :])))
                    for m in range(2):
                        lst.append(dict(kT=kT_na[:, pr, 2560 + 128 * m:2560 + 128 * (m + 1)], kbuf=KNB,
                                        v=v_na[:, 20 + m, h * 65:(h + 1) * 65], vbuf=VNB))
                    pre = None
                    if first:
                        first = False
                        if pr == 0:
                            pre = lambda: (load_tab(0), load_tab(1))
                        elif pr + 1 < 4:
                            pre = lambda pr=pr: load_tab(pr + 1)
                    calls.append(dict(qk=lst, q_ap=qn[p0:p0 + 64, pr, t0:t0 + qw], qw=qw,
                                      out_ap=qn[p0:p0 + 64, pr, t0:t0 + qw], buf=QNB[pr][u], part0=p0, pre=pre))
        attention_phase(calls)

    KGB, VGB = bf("kT_g"), bf("v_g")

    def phase_gqa_load():
        for r in range(4):
            k.dma("sp", d_kv, kT_g[:, r * 2048:(r + 1) * 2048], xout[0].ap()[r * 128:(r + 1) * 128, 0:2048],
                  reads=[bf("xout0")], writes=[KGB])
            k.dma("sp", d_kv, v_g[:, r * 16:(r + 1) * 16, :].rearrange("p t c -> p (t c)"),
                  xout[1].ap()[r * 128:(r + 1) * 128, :], reads=[bf("xout1")], writes=[VGB])
        k.seal(d_kv, [KGB, VGB])
        k.op("pool", lambda e: e.tensor_copy(out=kT_g[:, 8192:8448], in_=kst[:, 2048:2304]), reads=[KSB], writes=[KGB])
        k.op("pool", lambda e: e.tensor_copy(out=v_g[:, 64:66, :], in_=vst[:, 16:18, :]), reads=[VSB], writes=[VGB])

    def phase_gqa(l, qch, mod_next):
        att_init()
        att.update(nsg=3, look=2, obanks=(6, 7), bcbank=None, fdelay=10)
        calls = []
        for g in range(2):
            p0 = g * 64
            for rr in range(4):
                for u in qch:
                    t0, qw = CH[u]
                    kts = range(66) if u < 4 else range(64, 66)
                    lst = [dict(kT=kT_g[:, kt * 128:(kt + 1) * 128], kbuf=KGB,
                                v=v_g[:, kt, g * 65:(g + 1) * 65], vbuf=VGB) for kt in kts]
                    calls.append(dict(qk=lst, q_ap=qg[p0:p0 + 64, rr, t0:t0 + qw], qw=qw,
                                      out_ap=qg[p0:p0 + 64, rr, t0:t0 + qw], buf=QGB[rr][u], part0=p0))
        attention_phase(calls)

    def wload_m(srcs, shape):
        return wload(srcs, shape, "m")

    def phase_merge(l, blks):
        mb = cur["mb"]
        modT = cur["modT"]
        merged = fat("merged", [128, KC, 1280], BF16, 0)
        sga = at("sga", [128, 512], F32, W + 8192)
        zb = at("zb", [128, 512], F32, W + 19456)
        sgb_, zbb = bf("sga"), bf("zb")
        wl = w_in[l]
        for blk in blks:
            bt0 = CH[blk[0]][0]
            MB = {c: bf("merged_%d" % c) for c in blk}
            for fg in range(4):
                sl_ga, lb_ga = wload([(full, w_kc(wl, 2304 + fg * 256, 256))], [128, KC, 256], "ga")
                sl_pa, lb_pa = wload([(full, w_pa[l][:, fg * 256:(fg + 1) * 256].rearrange("(pr p) n -> p pr n", p=128))],
                                     [128, 4, 256], "pa")
                for fc2 in range(2):
                    fc = fg * 2 + fc2
                    for c in blk:
                        t0, tw = CH[c]
                        b1, b2 = next_ps(0, 8), next_ps(0, 8)
                        for kc in range(KC):
                            k.op("pe", lambda e, kc=kc, b1=b1, fc2=fc2, t0=t0, tw=tw: e.matmul(
                                ps[b1][:, 0:tw], lhsT=sl_ga[:, kc, fc2 * 128:(fc2 + 1) * 128], rhs=hT[:, kc, t0:t0 + tw],
                                start=(kc == 0), stop=(kc == KC - 1)), reads=[lb_ga, HB[c]], writes=[PSB[b1]])
                        for pr in range(4):
                            k.op("pe", lambda e, pr=pr, b2=b2, fc2=fc2, t0=t0, tw=tw: e.matmul(
                                ps[b2][:, 0:tw], lhsT=sl_pa[:, pr, fc2 * 128:(fc2 + 1) * 128], rhs=qn[:, pr, t0:t0 + tw],
                                start=(pr == 0), stop=(pr == 3)), reads=[lb_pa, QNB[pr][c]], writes=[PSB[b2]])
                        k.op("act", lambda e, b1=b1, tw=tw: e.activation(out=sga[:, 0:tw], in_=ps[b1][:, 0:tw], func=AF.Sigmoid),
                             reads=[PSB[b1]], writes=[sgb_])
                        k.op("dve", lambda e, b2=b2, fc=fc, t0=t0, tw=tw: e.tensor_tensor(
                            out=merged[:, fc, t0 - bt0:t0 - bt0 + tw], in0=sga[:, 0:tw], in1=ps[b2][:, 0:tw], op=ALU.mult),
                            reads=[sgb_, PSB[b2]], writes=[MB[c]])
                sl_gb, lb_gb = wload([(full, w_kc(wl, 3328 + fg * 256, 256))], [128, KC, 256], "gb")
                sl_pb, lb_pb = wload([(lambda v, g=g: v[64 * g:64 * g + 64, :, :],
                                       w_pb[l][g * 256:(g + 1) * 256, fg * 256:(fg + 1) * 256].rearrange("(rr d) n -> d rr n", d=64))
                                      for g in range(2)], [128, 4, 256], "pb")
                for fc2 in range(2):
                    fc = fg * 2 + fc2
                    for c in blk:
                        t0, tw = CH[c]
                        b1, b2 = next_ps(0, 8), next_ps(0, 8)
                        for kc in range(KC):
                            k.op("pe", lambda e, kc=kc, b1=b1, fc2=fc2, t0=t0, tw=tw: e.matmul(
                                ps[b1][:, 0:tw], lhsT=sl_gb[:, kc, fc2 * 128:(fc2 + 1) * 128], rhs=hT[:, kc, t0:t0 + tw],
                                start=(kc == 0), stop=(kc == KC - 1)), reads=[lb_gb, HB[c]], writes=[PSB[b1]])
                        for rr in range(4):
                            k.op("pe", lambda e, rr=rr, b2=b2, fc2=fc2, t0=t0, tw=tw: e.matmul(
                                ps[b2][:, 0:tw], lhsT=sl_pb[:, rr, fc2 * 128:(fc2 + 1) * 128], rhs=qg[:, rr, t0:t0 + tw],
                                start=(rr == 0), stop=(rr == 3)), reads=[lb_pb, QGB[rr][c]], writes=[PSB[b2]])
                        k.op("act", lambda e, b1=b1, tw=tw: e.activation(out=sga[:, 0:tw], in_=ps[b1][:, 0:tw], func=AF.Sigmoid),
                             reads=[PSB[b1]], writes=[sgb_])
                        k.op("dve", lambda e, b2=b2, tw=tw: e.tensor_tensor(
                            out=zb[:, 0:tw], in0=sga[:, 0:tw], in1=ps[b2][:, 0:tw], op=ALU.mult),
                            reads=[sgb_, PSB[b2]], writes=[zbb])
                        k.op("dve", lambda e, fc=fc, t0=t0, tw=tw: e.tensor_tensor(
                            out=merged[:, fc, t0 - bt0:t0 - bt0 + tw], in0=merged[:, fc, t0 - bt0:t0 - bt0 + tw],
                            in1=zb[:, 0:tw], op=ALU.add), reads=[zbb, MB[c]], writes=[MB[c]])
            for og in range(4):
                sl_o, lb_o = wload([(full, w_kc(w_o[l], og * 256, 256))], [128, KC, 256], "wo")
                for oc2 in range(2):
                    oc = og * 2 + oc2
                    for c in blk:
                        t0, tw = CH[c]
                        col = colsel(c)
                        b1 = next_ps(0, 8)
                        for fc in range(KC):
                            k.op("pe", lambda e, fc=fc, b1=b1, oc2=oc2, t0=t0, tw=tw: e.matmul(
                                ps[b1][:, 0:tw], lhsT=sl_o[:, fc, oc2 * 128:(oc2 + 1) * 128],
                                rhs=merged[:, fc, t0 - bt0:t0 - bt0 + tw], start=(fc == 0), stop=(fc == KC - 1)),
                                reads=[lb_o, MB[c]], writes=[PSB[b1]])
                        k.op("dve", lambda e, b1=b1, oc=oc, t0=t0, tw=tw, col=col: e.scalar_tensor_tensor(
                            out=xT[:, oc, t0:t0 + tw], in0=ps[b1][:, 0:tw], scalar=modT[:, 16 + oc, col:col + 1],
                            in1=xT[:, oc, t0:t0 + tw], op0=ALU.mult, op1=ALU.add),
                            reads=[PSB[b1], mb, XB[c]], writes=[XB[c]])

    def phase_ffn(l, blks):
        mb = cur["mb"]
        modT = cur["modT"]
        h2 = at("h2T", [128, KC, 1280], BF16, HT)
        gT = at("gT", [128, FC, 1280], BF16, HT + 20480)
        sa = [at("sa0", [128, 512], F32, FR + 3072 + 10240), at("sa1", [128, 512], F32, FR + 3072 + 12288)]
        sab = [bf("n2tmp0"), bf("n2tmp1")]
        for blk in blks:
            bt0 = CH[blk[0]][0]
            H2B = {c: bf("h2_%d" % c) for c in blk}
            GB = {c: bf("g_%d" % c) for c in blk}
            st = norm_stats(blk[0], FR + 3072, "n2")
            for ci_, c in enumerate(blk):
                nxt = norm_stats(blk[ci_ + 1], FR + 3072, "n2") if ci_ + 1 < len(blk) else None
                norm_apply(c, st, cur["A2"], 3, h2, CH[c][0] - bt0, H2B[c], FR + 3072, "n2")
                st = nxt
            for fg in range(FC // 2):
                sl_a, lb_a = wload([(full, w_kc(w_f1[l], fg * 256, 256))], [128, KC, 256], "fa")
                sl_u, lb_u = wload([(full, w_kc(w_f1[l], DFF + fg * 256, 256))], [128, KC, 256], "fu")
                for fc2 in range(2):
                    fc = fg * 2 + fc2
                    for c in blk:
                        t0, tw = CH[c]
                        o0 = t0 - bt0
                        b1, b2 = next_ps(0, 8), next_ps(0, 8)
                        for kc in range(KC):
                            k.op("pe", lambda e, kc=kc, b1=b1, fc2=fc2, o0=o0, tw=tw: e.matmul(
                                ps[b1][:, 0:tw], lhsT=sl_a[:, kc, fc2 * 128:(fc2 + 1) * 128], rhs=h2[:, kc, o0:o0 + tw],
                                start=(kc == 0), stop=(kc == KC - 1)), reads=[lb_a, H2B[c]], writes=[PSB[b1]])
                        for kc in range(KC):
                            k.op("pe", lambda e, kc=kc, b2=b2, fc2=fc2, o0=o0, tw=tw: e.matmul(
                                ps[b2][:, 0:tw], lhsT=sl_u[:, kc, fc2 * 128:(fc2 + 1) * 128], rhs=h2[:, kc, o0:o0 + tw],
                                start=(kc == 0), stop=(kc == KC - 1)), reads=[lb_u, H2B[c]], writes=[PSB[b2]])
                        si = fc % 2
                        k.op("act", lambda e, b1=b1, si=si, tw=tw: e.activation(out=sa[si][:, 0:tw], in_=ps[b1][:, 0:tw], func=AF.Silu),
                             reads=[PSB[b1]], writes=[sab[si]])
                        k.op("dve", lambda e, b2=b2, si=si, fc=fc, o0=o0, tw=tw: e.tensor_tensor(
                            out=gT[:, fc, o0:o0 + tw], in0=sa[si][:, 0:tw], in1=ps[b2][:, 0:tw], op=ALU.mult),
                            reads=[sab[si], PSB[b2]], writes=[GB[c]])
            for oc in range(KC):
                sl_o, lb_o = wload([(full, w_f2[l][:, oc * 128:(oc + 1) * 128].rearrange("(fc p) n -> p fc n", p=128))],
                                   [128, FC, 128], "f2")
                for c in blk:
                    t0, tw = CH[c]
                    o0 = t0 - bt0
                    col = colsel(c)
                    b1 = next_ps(0, 8)
                    for fc in range(FC):
                        k.op("pe", lambda e, fc=fc, b1=b1, o0=o0, tw=tw: e.matmul(
                            ps[b1][:, 0:tw], lhsT=sl_o[:, fc, :], rhs=gT[:, fc, o0:o0 + tw],
                            start=(fc == 0), stop=(fc == FC - 1)), reads=[lb_o, GB[c]], writes=[PSB[b1]])
                    k.op("dve", lambda e, b1=b1, oc=oc, t0=t0, tw=tw, col=col: e.scalar_tensor_tensor(
                        out=xT[:, oc, t0:t0 + tw], in0=ps[b1][:, 0:tw], scalar=modT[:, 40 + oc, col:col + 1],
                        in1=xT[:, oc, t0:t0 + tw], op0=ALU.mult, op1=ALU.add),
                        reads=[PSB[b1], mb, XB[c]], writes=[XB[c]])

    def phase_final():
        o = FR
        sq = at("sqf", [128, KC, 512], BF16, o)
        rstd = at("rstdf", [128, 512], F32, o + 8192)
        ob = [at("of0", [128, 512], F32, o + 10240), at("of1", [128, 512], F32, o + 12288)]
        sqb, rb = bf("sqf"), bf("rstdf")
        obb = [bf("of0"), bf("of1")]
        yb = bf("yout")
        yv = y_out.rearrange("(kc p) t -> p kc t", p=128)
        n = 0
        for c in range(4):
            t0, tw = CH[c]
            k.op("act", lambda e: e.activation(out=sq[:, :, 0:tw], in_=xT[:, :, t0:t0 + tw], func=AF.Square),
                 reads=[XB[c]], writes=[sqb])
            pb = 7
            for kc in range(KC):
                k.op("pe", lambda e, kc=kc: e.matmul(ps[pb][:, 0:tw], lhsT=ones_bf[:], rhs=sq[:, kc, 0:tw],
                                                     start=(kc == 0), stop=(kc == KC - 1)), reads=[sqb, cb], writes=[PSB[pb]])
            k.op("act", lambda e: e.activation(out=rstd[:, 0:tw], in_=ps[pb][:, 0:tw], func=AF.Sqrt, scale=1.0 / D,
                                               bias=eps_t[:, 0:1]), reads=[PSB[pb], cb], writes=[rb])
            k.op("dve", lambda e: e.reciprocal(out=rstd[:, 0:tw], in_=rstd[:, 0:tw]), reads=[rb], writes=[rb])
            for kc in range(KC):
                i = n % 2
                n += 1
                k.op("dve", lambda e, kc=kc, i=i: e.scalar_tensor_tensor(
                    out=ob[i][:, 0:tw], in0=xT[:, kc, t0:t0 + tw], scalar=fnT[:, kc:kc + 1], in1=rstd[:, 0:tw],
                    op0=ALU.mult, op1=ALU.mult), reads=[XB[c], cb, rb], writes=[obb[i]])
                k.dma("sp", d_out, yv[:, kc, t0:t0 + tw], ob[i][:, 0:tw], reads=[obb[i]], writes=[yb])
        k._wait(k.E["sp"], [yb], ())

    def dump(name, t_ap, npart, n):
        o = nc.dram_tensor("dbg_" + name, [npart, n], F32, kind="ExternalOutput").ap()
        dbg_names.append("dbg_" + name)
        db = bf("dbg_" + name)
        k.barrier()
        k.dma("pool", d_out, o, t_ap, writes=[db])
        k._wait(k.E["pool"], [db], ())

    def dbg_at(name, l):
        return debug is not None and debug[0] == name and debug[1] == l

    done = False
    fl = lambda t: t[:].rearrange("p a b -> p (a b)")
    mod_use_alt(False)
    phase_mod(0)
    for l in range(n_layers):
        last = (l == DEPTH - 1)
        qch = [0, 1, 2, 3] if last else [0, 1, 2, 3, 4]
        blks = [[0, 1], [2, 3]] if last else BLK
        set_layer(l)
        phase_norm1(l)
        if dbg_at("h", l):
            dump("h", fl(hT), 128, 8 * TT); done = True; break
        phase_spill()
        k.barrier()
        phase_proj_kv(l)
        phase_exchange()
        phase_proj_q(l, qch)
        k.barrier()
        if dbg_at("proj", l):
            dump("qn", fl(qn), 128, 4 * TT); dump("qg", fl(qg), 128, 4 * TT); dump("kn", fl(kT_na), 128, 4 * 2816)
            dump("vn", fl(v_na), 128, 22 * 520); dump("kst", kst[:], 128, TT); dump("vst", fl(vst), 128, 18 * 130)
            done = True; break
        phase_halo()
        k.barrier()
        if dbg_at("halo", l):
            dump("kn", fl(kT_na), 128, 4 * 2816); dump("vn", fl(v_na), 128, 22 * 520); done = True; break
        mod_next = l + 1 if l + 1 < n_layers else None
        phase_na(l, qch, mod_next)
        att["hook"] = None
        if mod_next is not None:
            while na_state["st"]["g"] < 24:
                na_state["hook"]()
            mod_finish(mod_next)
        k.barrier()
        if dbg_at("na", l):
            dump("yn", fl(qn), 128, 4 * TT); done = True; break
        phase_gqa_load()
        phase_gqa(l, qch, mod_next)
        k.barrier()
        if dbg_at("gqa", l):
            dump("yn", fl(qn), 128, 4 * TT); dump("yg", fl(qg), 128, 4 * TT); done = True; break
        phase_restore()
        phase_merge(l, blks)
        k.barrier()
        if dbg_at("merge", l):
            dump("x", fl(xT), 128, 8 * TT); done = True; break
        phase_ffn(l, blks)
        k.barrier()
        if dbg_at("ffn", l):
            dump("x", fl(xT), 128, 8 * TT); done = True; break
    if not done:
        phase_final()
    print("program: n_ins=%d n_wait=%d" % (k.n_ins, k.n_wait))
    nc._dbg_names = dbg_names
    return nc


def _rope_tables(n_tokens):
    t = np.arange(n_tokens)
    row = (t // GRID_W).astype(np.float32)
    col = (t % GRID_W).astype(np.float32)
    half = HD // 2
    inv = (np.float32(10000.0) ** (-np.arange(0, half, 2, dtype=np.float32) / np.float32(half))).astype(np.float32)
    ang = np.concatenate([row[:, None] * inv, col[:, None] * inv], axis=-1).astype(np.float32)
    return np.cos(ang).astype(np.float32), np.sin(ang).astype(np.float32)


def _bias_tables(na_rpb):
    p = np.arange(128)
    ck = p % 64
    sub = p // 64
    i = np.arange(22)
    cq = np.arange(64)
    dr = (17 + sub)[:, None] - i[None, :]
    dr_ok = (dr >= 0) & (dr <= 14)
    drc = np.clip(dr, 0, 14)
    dc = np.clip(ck[:, None] - cq[None, :], -15, 15) + 15
    c_start = np.clip(cq - 8, 0, GRID_W - 16)
    inwin = (ck[:, None] >= c_start[None, :]) & (ck[:, None] < c_start[None, :] + 16)
    out = np.zeros((DEPTH, 4, 128, 2, 22, 64), np.float32)
    for pr in range(4):
        for hh in range(2):
            h = 2 * pr + hh
            g = na_rpb[:, h][:, drc[:, :, None], dc[:, None, :]]
            g = np.where(dr_ok[None, :, :, None], g, np.float32(0.0))
            g = np.where(inwin[None, :, None, :], g, np.float32(NEG))
            out[:, pr, :, hh] = g
    return out.reshape(DEPTH, 4, 128, 2 * 22 * 64)


def _pen_table(j):
    rows = 128
    R0 = 32 * j
    pen = np.zeros((128, 4, 512), np.float32)
    pen[0:16] = NEG
    for u in range(4):
        for rq in range(8):
            R = R0 + 8 * u + rq
            rs = min(max(R - 4, 0), rows - 8)
            for m in range(16):
                gr = R0 + 8 * u - 4 + m
                if rs <= gr < rs + 8:
                    pen[m, u, rq * 64:(rq + 1) * 64] = 0.0
    return pen.reshape(128, 4 * 512)


def _consts():
    perm = np.zeros((128, 128), np.float32)
    for i in range(64):
        perm[2 * i + 1, 2 * i] = -1.0
        perm[2 * i, 2 * i + 1] = 1.0
    onehot = np.zeros((128, 8, 128), np.float32)
    for kt in range(8):
        for key in range(128):
            onehot[2 * kt + key // 64, kt, key] = 1.0
    return perm, onehot.reshape(128, 8 * 128)


_CACHE = {}
_N_LAYERS = DEPTH
_DEBUG = None


def kernel(x, c, ctx, c_ctx, w_mod, b_mod, norm1, w_in, na_rpb, q_gain, k_gain, w_pa, w_pb, w_o,
           norm2, w_ffn_in, w_ffn_out, final_norm):
    f = lambda a: np.ascontiguousarray(np.asarray(a, dtype=np.float32))
    x, c, ctx, c_ctx = f(x), f(c), f(ctx), f(c_ctx)
    w_mod, b_mod, norm1, w_in, na_rpb = f(w_mod), f(b_mod), f(norm1), f(w_in), f(na_rpb)
    q_gain, k_gain, w_pa, w_pb, w_o = f(q_gain), f(k_gain), f(w_pa), f(w_pb), f(w_o)
    norm2, w_ffn_in, w_ffn_out, final_norm = f(norm2), f(w_ffn_in), f(w_ffn_out), f(final_norm)

    if "nc" not in _CACHE:
        _CACHE["nc"] = build_program(n_layers=_N_LAYERS, debug=_DEBUG)
        _CACHE["dbg_names"] = list(getattr(_CACHE["nc"], "_dbg_names", []))
    nc = _CACHE["nc"]
    NL = _N_LAYERS

    fm = lambda v: np.ascontiguousarray(v.reshape(KC, 128).T)
    norm1T = np.ascontiguousarray(norm1.reshape(DEPTH, KC, 128).transpose(2, 0, 1))
    norm2T = np.ascontiguousarray(norm2.reshape(DEPTH, KC, 128).transpose(2, 0, 1))
    bmodT = np.ascontiguousarray(b_mod.reshape(DEPTH, 48, 128).transpose(2, 0, 1))
    fnT = fm(final_norm)
    pidx = np.arange(128) % 64
    qgT = np.ascontiguousarray(q_gain[:, pidx].T)
    kgT = np.ascontiguousarray(k_gain[:, pidx].T)
    cos_all, sin_all = _rope_tables(8192)
    perm, onehot = _consts()
    tb = _bias_tables(na_rpb)
    shared = dict(ident=np.eye(128, dtype=np.float32), norm1T=norm1T, norm2T=norm2T, bmodT=bmodT, fnT=fnT, qgainT=qgT, kgainT=kgT, perm=perm,
                  onehot=onehot, tb=tb, w_mod=w_mod[:NL], w_in=w_in[:NL], w_pa=w_pa[:NL], w_pb=w_pb[:NL],
                  w_o=w_o[:NL], w_ffn_in=w_ffn_in[:NL], w_ffn_out=w_ffn_out[:NL])
    in_maps = []
    for i in range(NCORES):
        b, j = i // 4, i % 4
        xs = x[b, j * T_OWN:(j + 1) * T_OWN]
        xT = np.ascontiguousarray(np.concatenate([xs.T, ctx[b].T], axis=1))
        cvec = np.ascontiguousarray(np.stack([fm(c[b]), fm(c_ctx)], axis=-1))
        tsl = slice(j * T_OWN, (j + 1) * T_OWN)
        pair = (np.arange(128) % 64) // 2
        cosT = np.ascontiguousarray(cos_all[tsl][:, pair].T)
        sinT = np.ascontiguousarray(sin_all[tsl][:, pair].T)
        sel = np.zeros((128, 8), np.float32)
        if j > 0:
            sel[:, j - 1] = 1.0
        if j < 3:
            sel[:, 4 + j + 1] = 1.0
        m = dict(shared)
        m.update(xT_in=xT, cvec=cvec, cosT=cosT, sinT=sinT, sel=sel, pen=_pen_table(j))
        in_maps.append(m)
    res = run_bass_kernel_spmd(nc, in_maps, core_ids=list(range(NCORES)))
    if _DEBUG is not None:
        return {n: np.stack([res.results[i][n] for i in range(NCORES)]) for n in _CACHE["dbg_names"]}
    out = np.empty((2, 8192, D), np.float32)
    for i in range(NCORES):
        b, j = i // 4, i % 4
        out[b, j * T_OWN:(j + 1) * T_OWN] = res.results[i]["yT_out"].T
    return out
```
